# Optimizing a Trainium2 kernel written in Bass

```python
import math
import jax, jax.numpy as jnp
from jax import lax
import numpy as np

D_MODEL = 1024
BATCH = 2
SEQ = 8192
DEPTH = 4
DEC_BATCH = 8
DEC_SEQ = 32
PAST_LEN = 1024

CHUNK = 64
N_MIXERS = 2
N_A_LAYERS = (DEPTH + 1) // 2
N_B_LAYERS = DEPTH // 2
D_MIX = D_MODEL
S5_GROUP = 16
S5_GROUPS = D_MIX // S5_GROUP
S5_STATE = 64
POOL_WINDOWS = (2, 4, 8, 16)
POOL_GROUPS = len(POOL_WINDOWS)
POOL_GW = D_MIX // POOL_GROUPS
POOL_HIST = max(POOL_WINDOWS) - 1
D_FF = 2816
N_MEM = 256
N_XHEADS = 4
XHEAD_DIM = D_MODEL // N_XHEADS
N_SUB = 4
RMS_EPS = 1e-6

kernel_name = "s5_pool_macaron_stream_step"


def rmsnorm(x, g):
    xf = x.astype(jnp.float32)
    y = xf * lax.rsqrt(jnp.mean(xf * xf, axis=-1, keepdims=True) + RMS_EPS)
    return (y * g.astype(jnp.float32)).astype(x.dtype)


def swiglu(h, w_gate, w_up, w_down):
    return (jax.nn.silu(h @ w_gate) * (h @ w_up)) @ w_down


def s5_discretise(a_re, a_im, b_re, b_im, log_dt):
    f32 = jnp.float32
    lam = lax.complex(a_re.astype(f32), a_im.astype(f32))
    dt = jnp.exp(log_dt.astype(f32))[:, None]
    lam_bar = jnp.exp(lam * dt)
    b = lax.complex(b_re.astype(f32), b_im.astype(f32))
    b_bar = ((lam_bar - 1.0) / lam)[:, :, None] * b
    return lam_bar, b_bar


def _linear_combine(left, right):
    a_l, b_l = left
    a_r, b_r = right
    return a_r * a_l, a_r * b_l + b_r


def s5_block(h0, u_blk, lam_bar, b_bar, c):
    bu = jnp.einsum('gpc,btgc->btgp', b_bar, u_blk.astype(jnp.complex64))
    a = jnp.broadcast_to(lam_bar, bu.shape)
    a_cum, h_loc = lax.associative_scan(_linear_combine, (a, bu), axis=1)
    h = a_cum * h0[:, None] + h_loc
    y = jnp.einsum('gcp,btgp->btgc', c, h).real
    return h[:, -1], y


def s5_mixer(u, h0, a_re, a_im, b_re, b_im, c_re, c_im, d, log_dt, w_glu):
    f32 = jnp.float32
    bsz, t, _ = u.shape
    lam_bar, b_bar = s5_discretise(a_re, a_im, b_re, b_im, log_dt)
    c = lax.complex(c_re.astype(f32), c_im.astype(f32))
    ug = u.astype(f32).reshape(bsz, t, S5_GROUPS, S5_GROUP)
    if t > CHUNK:
        nc = t // CHUNK
        blocks = ug.reshape(bsz, nc, CHUNK, S5_GROUPS, S5_GROUP).transpose(1, 0, 2, 3, 4)

        def step(h, ub):
            return s5_block(h, ub, lam_bar, b_bar, c)

        h_last, ys = lax.scan(step, h0, blocks)
        y = ys.transpose(1, 0, 2, 3, 4).reshape(bsz, t, D_MIX)
    else:
        h_last, y = s5_block(h0, ug, lam_bar, b_bar, c)
        y = y.reshape(bsz, t, D_MIX)
    y = y + d.astype(f32) * u.astype(f32)
    z = jax.nn.gelu(y).astype(u.dtype)
    gate = z @ w_glu
    out = gate[..., :D_MODEL] * jax.nn.sigmoid(gate[..., D_MODEL:])
    return out, h_last


def pool_mixer(u, hist, pos0, w_grp, b_grp, scale):
    f32 = jnp.float32
    bsz, t, _ = u.shape
    ext = jnp.concatenate([hist.astype(u.dtype), u], axis=1)
    cs = jnp.cumsum(ext.astype(f32), axis=1)
    cs = jnp.concatenate([jnp.zeros((bsz, 1, D_MIX), f32), cs], axis=1)
    pos = pos0 + jnp.arange(t, dtype=jnp.int32)
    end = cs[:, POOL_HIST + 1:POOL_HIST + 1 + t]
    parts = []
    for g, w in enumerate(POOL_WINDOWS):
        sl = slice(g * POOL_GW, (g + 1) * POOL_GW)
        start = cs[:, POOL_HIST + 1 - w:POOL_HIST + 1 - w + t, sl]
        cnt = jnp.minimum(pos + 1, w).astype(f32)[None, :, None]
        parts.append((end[..., sl] - start) / cnt)
    pooled = jnp.concatenate(parts, axis=-1) - u.astype(f32)
    pooled = pooled.reshape(bsz, t, POOL_GROUPS, POOL_GW).astype(u.dtype)
    mixed = jnp.einsum('btgi,gio->btgo', pooled, w_grp) + b_grp
    out = mixed.reshape(bsz, t, D_MIX) * scale
    return out, ext[:, -POOL_HIST:]


def cross_attend(h, k, v, w_q, w_o):
    bsz, t, _ = h.shape
    q = (h @ w_q).reshape(bsz, t, N_XHEADS, XHEAD_DIM)
    s = jnp.einsum('bthd,bmhd->bhtm', q, k).astype(jnp.float32) * (XHEAD_DIM ** -0.5)
    p = jax.nn.softmax(s, axis=-1).astype(v.dtype)
    o = jnp.einsum('bhtm,bmhd->bthd', p, v).reshape(bsz, t, D_MODEL)
    return o @ w_o


def trunk(x, pos0, mem_k, mem_v, ssm_h0, pool_hist, p):
    new_h, new_buf = [], []
    for i in range(DEPTH):
        j = i // N_MIXERS
        h = rmsnorm(x, p['norm_pre'][i, 0])
        f = swiglu(h, p['ffn_w_gate'][i, 0], p['ffn_w_up'][i, 0], p['ffn_w_down'][i, 0])
        x = x + 0.5 * rmsnorm(f, p['norm_post'][i, 0])
        h = rmsnorm(x, p['norm_pre'][i, 1])
        if i % N_MIXERS == 0:
            mix, hs = s5_mixer(h @ p['s5_w_in'][j], ssm_h0[j], p['s5_a_re'][j], p['s5_a_im'][j],
                               p['s5_b_re'][j], p['s5_b_im'][j], p['s5_c_re'][j], p['s5_c_im'][j],
                               p['s5_d'][j], p['s5_log_dt'][j], p['s5_w_glu'][j])
            new_h.append(hs)
        else:
            mix, buf = pool_mixer(h @ p['pool_w_in'][j], pool_hist[j], pos0,
                                  p['pool_w_grp'][j], p['pool_b_grp'][j], p['pool_scale'][j])
            new_buf.append(buf)
        x = x + rmsnorm(mix, p['norm_post'][i, 1])
        h = rmsnorm(x, p['norm_pre'][i, 2])
        a = cross_attend(h, mem_k[i], mem_v[i], p['xa_w_q'][i], p['xa_w_o'][i])
        x = x + rmsnorm(a, p['norm_post'][i, 2])
        h = rmsnorm(x, p['norm_pre'][i, 3])
        f = swiglu(h, p['ffn_w_gate'][i, 1], p['ffn_w_up'][i, 1], p['ffn_w_down'][i, 1])
        x = x + 0.5 * rmsnorm(f, p['norm_post'][i, 3])
    return x, jnp.stack(new_h), jnp.stack(new_buf)


def setup_inputs(seed: int = 0) -> dict:
    key = jax.random.key(seed)
    ks = iter(jax.random.split(key, 40))
    f32 = jnp.float32

    def nrm(shape, scale=1.0):
        return jax.random.normal(next(ks), shape, f32) * scale

    n_idx = jnp.arange(S5_STATE, dtype=f32)
    inputs = {}
    inputs['x_prompt'] = nrm((BATCH, SEQ, D_MODEL))
    inputs['x_sample'] = nrm((DEC_BATCH, DEC_SEQ, D_MODEL))
    inputs['cache_mem_k'] = nrm((DEPTH, DEC_BATCH, N_MEM, N_XHEADS, XHEAD_DIM))
    inputs['cache_mem_v'] = nrm((DEPTH, DEC_BATCH, N_MEM, N_XHEADS, XHEAD_DIM))
    inputs['state_ssm_re'] = nrm((N_A_LAYERS, DEC_BATCH, S5_GROUPS, S5_STATE), 0.5)
    inputs['state_ssm_im'] = nrm((N_A_LAYERS, DEC_BATCH, S5_GROUPS, S5_STATE), 0.5)
    inputs['state_pool'] = nrm((N_A_LAYERS * 0 + N_B_LAYERS, DEC_BATCH, POOL_HIST, D_MIX))
    inputs['mem_prompt'] = nrm((BATCH, N_MEM, D_MODEL))
    inputs['norm_pre'] = 1.0 + nrm((DEPTH, N_SUB, D_MODEL), 0.05)
    inputs['norm_post'] = 1.0 + nrm((DEPTH, N_SUB, D_MODEL), 0.05)
    inputs['norm_mem'] = 1.0 + nrm((DEPTH, D_MODEL), 0.05)
    inputs['ffn_w_gate'] = nrm((DEPTH, 2, D_MODEL, D_FF), D_MODEL ** -0.5)
    inputs['ffn_w_up'] = nrm((DEPTH, 2, D_MODEL, D_FF), D_MODEL ** -0.5)
    inputs['ffn_w_down'] = nrm((DEPTH, 2, D_FF, D_MODEL), D_FF ** -0.5)
    inputs['xa_w_q'] = nrm((DEPTH, D_MODEL, D_MODEL), D_MODEL ** -0.5)
    inputs['xa_w_k'] = nrm((DEPTH, D_MODEL, D_MODEL), D_MODEL ** -0.5)
    inputs['xa_w_v'] = nrm((DEPTH, D_MODEL, D_MODEL), D_MODEL ** -0.5)
    inputs['xa_w_o'] = nrm((DEPTH, D_MODEL, D_MODEL), D_MODEL ** -0.5)
    inputs['s5_w_in'] = nrm((N_A_LAYERS, D_MODEL, D_MIX), D_MODEL ** -0.5)
    inputs['s5_a_re'] = -0.5 + nrm((N_A_LAYERS, S5_GROUPS, S5_STATE), 0.01)
    inputs['s5_a_im'] = jnp.broadcast_to(math.pi * n_idx, (N_A_LAYERS, S5_GROUPS, S5_STATE)) + nrm((N_A_LAYERS, S5_GROUPS, S5_STATE), 0.01)
    inputs['s5_b_re'] = nrm((N_A_LAYERS, S5_GROUPS, S5_STATE, S5_GROUP), (2 * S5_GROUP) ** -0.5)
    inputs['s5_b_im'] = nrm((N_A_LAYERS, S5_GROUPS, S5_STATE, S5_GROUP), (2 * S5_GROUP) ** -0.5)
    inputs['s5_c_re'] = nrm((N_A_LAYERS, S5_GROUPS, S5_GROUP, S5_STATE), (2 * S5_STATE) ** -0.5)
    inputs['s5_c_im'] = nrm((N_A_LAYERS, S5_GROUPS, S5_GROUP, S5_STATE), (2 * S5_STATE) ** -0.5)
    inputs['s5_d'] = nrm((N_A_LAYERS, D_MIX))
    inputs['s5_log_dt'] = jax.random.uniform(next(ks), (N_A_LAYERS, S5_GROUPS), f32, math.log(1e-3), math.log(1e-1))
    inputs['s5_w_glu'] = nrm((N_A_LAYERS, D_MIX, 2 * D_MODEL), D_MIX ** -0.5)
    inputs['pool_w_in'] = nrm((N_B_LAYERS, D_MODEL, D_MIX), D_MODEL ** -0.5)
    inputs['pool_w_grp'] = nrm((N_B_LAYERS, POOL_GROUPS, POOL_GW, POOL_GW), POOL_GW ** -0.5)
    inputs['pool_b_grp'] = nrm((N_B_LAYERS, POOL_GROUPS, POOL_GW), 0.02)
    inputs['pool_scale'] = 1.0 + nrm((N_B_LAYERS, D_MIX), 0.1)
    return inputs


def reference(x_prompt, x_sample, cache_mem_k, cache_mem_v, state_ssm_re, state_ssm_im, state_pool, mem_prompt,
              norm_pre, norm_post, norm_mem, ffn_w_gate, ffn_w_up, ffn_w_down,
              xa_w_q, xa_w_k, xa_w_v, xa_w_o,
              s5_w_in, s5_a_re, s5_a_im, s5_b_re, s5_b_im, s5_c_re, s5_c_im, s5_d, s5_log_dt, s5_w_glu,
              pool_w_in, pool_w_grp, pool_b_grp, pool_scale):
    f32 = jnp.float32
    p = dict(norm_pre=norm_pre, norm_post=norm_post, ffn_w_gate=ffn_w_gate, ffn_w_up=ffn_w_up,
             ffn_w_down=ffn_w_down, xa_w_q=xa_w_q, xa_w_o=xa_w_o, s5_w_in=s5_w_in, s5_a_re=s5_a_re,
             s5_a_im=s5_a_im, s5_b_re=s5_b_re, s5_b_im=s5_b_im, s5_c_re=s5_c_re, s5_c_im=s5_c_im,
             s5_d=s5_d, s5_log_dt=s5_log_dt, s5_w_glu=s5_w_glu, pool_w_in=pool_w_in,
             pool_w_grp=pool_w_grp, pool_b_grp=pool_b_grp, pool_scale=pool_scale)
    bsz = x_prompt.shape[0]

    m = rmsnorm(mem_prompt[None], norm_mem[:, None, None, :])
    mem_k_prompt = jnp.einsum('lbmd,lde->lbme', m, xa_w_k).reshape(DEPTH, bsz, N_MEM, N_XHEADS, XHEAD_DIM)
    mem_v_prompt = jnp.einsum('lbmd,lde->lbme', m, xa_w_v).reshape(DEPTH, bsz, N_MEM, N_XHEADS, XHEAD_DIM)
    h0_prompt = jnp.zeros((N_A_LAYERS, bsz, S5_GROUPS, S5_STATE), jnp.complex64)
    hist_prompt = jnp.zeros((N_B_LAYERS, bsz, POOL_HIST, D_MIX), x_prompt.dtype)
    y_prompt, h_p, buf_p = trunk(x_prompt, 0, mem_k_prompt, mem_v_prompt, h0_prompt, hist_prompt, p)

    h0_sample = lax.complex(state_ssm_re.astype(f32), state_ssm_im.astype(f32))
    y_sample, h_s, buf_s = trunk(x_sample, PAST_LEN, cache_mem_k, cache_mem_v, h0_sample, state_pool, p)

    sdt = state_ssm_re.dtype
    return (y_prompt, y_sample, mem_k_prompt, mem_v_prompt,
            jnp.real(h_p).astype(sdt), jnp.imag(h_p).astype(sdt), buf_p,
            jnp.real(h_s).astype(sdt), jnp.imag(h_s).astype(sdt), buf_s)
```

```python
import math
import os
KD = int(os.environ.get('KDEBUG', '99'))
from contextlib import ExitStack
import numpy as np
import concourse.bass as bass
import concourse.mybir as mybir
from concourse.bass_utils import run_bass_kernel_spmd

F32 = mybir.dt.float32
BF16 = mybir.dt.bfloat16
I32 = mybir.dt.int32
AF = mybir.ActivationFunctionType
ALU = mybir.AluOpType

ENGS = ("pe", "act", "dve", "pool", "sp")
D = 1024
DFF = 2816
NTP = 2048
NTS = 32
NT = NTP + NTS
TILES = [(0, 512), (512, 512), (1024, 512), (1536, 512), (2048, 32)]
HALVES = [[0, 1], [2, 3, 4]]
S5T = 128
EPS = 1e-6
TWO_PI = 2.0 * math.pi


class Op:
    __slots__ = ("idx", "eng", "fn", "deps", "is_dma", "sig", "tick", "dsem", "dval", "dinc")

    def __init__(self, idx, eng, fn, is_dma):
        self.idx = idx
        self.eng = eng
        self.fn = fn
        self.deps = set()
        self.is_dma = is_dma
        self.sig = False
        self.tick = None


class Prog:
    def __init__(self, nc, n_dma_sems=48, same_engine_sync=True):
        self.nc = nc
        self.ops = []
        self.last_w = {}
        self.readers = {}
        self.n_dma_sems = n_dma_sems
        self.same_engine_sync = same_engine_sync
        self.last_eng = {}
        self.open_dma = []
        self.pending_barrier = {}

    def op(self, eng, fn, r=(), w=(), dma=False):
        o = Op(len(self.ops), eng, fn, dma)
        for k in r:
            lw = self.last_w.get(k)
            if lw is not None:
                o.deps.add(lw)
        for k in w:
            lw = self.last_w.get(k)
            if lw is not None:
                o.deps.add(lw)
            rd = self.readers.get(k)
            if rd:
                o.deps.update(rd.values())
        for k in r:
            rd = self.readers.setdefault(k, {})
            rd[("d", o.idx) if dma else eng] = o.idx
        for k in w:
            self.last_w[k] = o.idx
            self.readers[k] = {}
        pb = self.pending_barrier.pop(eng, None)
        if pb:
            o.deps.update(pb)
        o.deps.discard(o.idx)
        self.ops.append(o)
        if dma:
            self.open_dma.append(o.idx)
        else:
            self.last_eng[eng] = o.idx
        return o

    def barrier(self):
        b = set(self.last_eng.values()) | set(self.open_dma)
        self.open_dma = []
        for e in ENGS:
            self.pending_barrier[e] = set(b) | self.pending_barrier.get(e, set())

    def emit(self):
        nc = self.nc
        ops = self.ops
        for o in ops:
            nd = set()
            for d in o.deps:
                p = ops[d]
                if (not p.is_dma) and p.eng == o.eng:
                    if o.eng == "pe" or not self.same_engine_sync:
                        continue
                nd.add(d)
            o.deps = nd
            for d in nd:
                ops[d].sig = True
        cnt = {e: 0 for e in ENGS}
        for o in ops:
            if o.is_dma:
                continue
            if o.sig:
                cnt[o.eng] += 1
                o.tick = cnt[o.eng]
        dma_ops = [o for o in ops if o.is_dma]
        nds = self.n_dma_sems
        dcount = [0] * nds
        pools = {"sw": list(range(0, 20)), "hw": list(range(20, nds - 2)), "cc": list(range(nds - 2, nds))}
        rr = {"sw": 0, "hw": 0, "cc": 0}
        for o in dma_ops:
            kind = "cc" if o.is_dma == "cc" else ("sw" if o.eng == "pool" else "hw")
            s = pools[kind][rr[kind] % len(pools[kind])]
            rr[kind] += 1
            o.dsem = s
            o.dinc = 1 if o.is_dma == "cc" else 16
            dcount[s] += o.dinc
            o.dval = dcount[s]
        with ExitStack() as es:
            esem = {e: es.enter_context(nc.semaphore("tick_" + e)) for e in ENGS}
            dsems = [es.enter_context(nc.semaphore("dma_%d" % i)) for i in range(nds)]
            block = es.enter_context(nc.Block())
            per_eng = {e: [o for o in ops if o.eng == e] for e in ENGS}

            def make(e):
                def body(eng):
                    seen = {}
                    for o in per_eng[e]:
                        need = {}
                        for d in o.deps:
                            p = ops[d]
                            if p.is_dma:
                                k = ("d", p.dsem)
                                v = p.dval
                            else:
                                k = ("e", p.eng)
                                v = p.tick
                            if need.get(k, 0) < v:
                                need[k] = v
                        if o.is_dma and o.dval > o.dinc:
                            k = ("d", o.dsem)
                            v = o.dval - o.dinc
                            if need.get(k, 0) < v:
                                need[k] = v
                        for k, v in need.items():
                            if seen.get(k, 0) >= v:
                                continue
                            seen[k] = v
                            eng.wait_ge(dsems[k[1]] if k[0] == "d" else esem[k[1]], v)
                        ins = o.fn(eng)
                        if o.is_dma == "cc":
                            ins.then_inc(dsems[o.dsem])
                        elif o.is_dma:
                            ins.then_inc(dsems[o.dsem], 16)
                        elif o.sig:
                            ins.then_inc(esem[e], 1)
                return body

            block.tensor(make("pe"))
            block.scalar(make("act"))
            block.vector(make("dve"))
            block.gpsimd(make("pool"))
            block.sync(make("sp"))
        return cnt


def _dsize(dt):
    return 2 if dt == BF16 else 4


class Arena:
    def __init__(self, nc, lo, hi):
        self.nc = nc
        self.lo = lo
        self.hi = hi
        self.off = lo
        self.n = 0

    def alloc(self, name, shape, dtype):
        size = _dsize(dtype)
        for s in shape[1:]:
            size *= s
        size = (size + 63) // 64 * 64
        assert self.off + size <= self.hi, (name, self.off, size, self.hi)
        self.n += 1
        t = self.nc.alloc_sbuf_tensor_at("%s_%d" % (name, self.n), list(shape), dtype, offset=self.off)
        self.off += size
        return t

    def mark(self):
        return self.off

    def reset(self, m):
        self.off = m


class Builder:
    def __init__(self, stop_after=None):
        self.stop_after = stop_after
        nc = bass.Bass("TRN2", target_bir_lowering=False)
        self.nc = nc
        self.P = Prog(nc)
        lo = (nc.sbuf_base + 63) // 64 * 64
        self.A = Arena(nc, lo, nc.sbuf_top)
        self.uid = 0
        self.ins = {}
        self.outs = {}
        self.banks = [nc.alloc_psum_tensor("bank%d" % i, [128, 512], F32) for i in range(8)]
        self.bank_rr = 0
        self.out_keys = []

    def din(self, name, shape):
        t = self.nc.dram_tensor(name, list(shape), F32, kind="ExternalInput").ap()
        self.ins[name] = t
        return t

    def dout(self, name, shape):
        t = self.nc.dram_tensor(name, list(shape), F32, kind="ExternalOutput").ap()
        self.outs[name] = t
        return t

    def key(self, base):
        self.uid += 1
        return "%s#%d" % (base, self.uid)

    def bank(self, nb=6):
        b = self.bank_rr % nb
        self.bank_rr += 1
        return b

    def store(self, dst_ap, src_ap, rkeys, eng="sp"):
        k = self.key("out")
        self.out_keys.append(k)
        self.P.op(eng, lambda e: e.dma_start(out=dst_ap, in_=src_ap), r=rkeys, w=[k], dma=True)

    def build(self):
        nc, P, A = self.nc, self.P, self.A
        op = P.op
        xp = self.din("xp", [NTP, D])
        xs = self.din("xs", [NTS, D])
        ck = self.din("ck", [4, 256, D])
        cv = self.din("cv", [4, 256, D])
        sre = self.din("sre", [2, 32, 128])
        sim = self.din("sim", [2, 32, 128])
        spool = self.din("spool", [2, 15, D])
        memp = self.din("memp", [256, D])
        vecs = self.din("vecs", [42, D])
        cmask = self.din("cmask", [128, 40])
        wg = self.din("ffn_w_gate", [4, 2, D, DFF])
        wu = self.din("ffn_w_up", [4, 2, D, DFF])
        wd = self.din("ffn_w_down", [4, 2, DFF, D])
        wq = self.din("xa_w_q", [4, D, D])
        wk = self.din("xa_w_k", [4, D, D])
        wv = self.din("xa_w_v", [4, D, D])
        wo = self.din("xa_w_o", [4, D, D])
        s5win = self.din("s5_w_in", [2, D, D])
        s5are = self.din("s5_a_re", [2, 32, 128])
        s5aim = self.din("s5_a_im", [2, 32, 128])
        s5bre = self.din("s5_b_re", [2, 64, 64, 16])
        s5bim = self.din("s5_b_im", [2, 64, 64, 16])
        s5cre = self.din("s5_c_re", [2, 64, 16, 64])
        s5cim = self.din("s5_c_im", [2, 64, 16, 64])
        s5ldt = self.din("s5_log_dt", [2, 2, 32])
        s5glu = self.din("s5_w_glu", [2, D, 2 * D])
        pwin = self.din("pool_w_in", [2, D, D])
        pwgrp = self.din("pool_w_grp", [2, 4, 256, 256])

        yp = self.dout("yp", [NTP, D])
        ys = self.dout("ys", [NTS, D])
        mk = self.dout("mk", [4, 256, D])
        mv = self.dout("mv", [4, 256, D])
        o_hre = self.dout("o_hre", [2, 32, 128])
        o_him = self.dout("o_him", [2, 32, 128])
        o_pool = self.dout("o_pool", [2, 15, D])
        o_shre = self.dout("o_shre", [2, 32, 128])
        o_shim = self.dout("o_shim", [2, 32, 128])
        o_spool = self.dout("o_spool", [2, 15, D])
        g_in = nc.dram_tensor("g_in", [128, 128], F32)
        g_out = nc.dram_tensor("g_out", [8 * 128, 128], F32)
        self.g_in, self.g_out = g_in, g_out

        self.xT = A.alloc("xT", [128, 8, NT], F32)
        self.ident = A.alloc("ident", [128, 128], F32)
        self.ones = A.alloc("ones", [128, 128], BF16)
        self.vT = A.alloc("vT", [128, 8, 42], F32)
        self.cm = A.alloc("cm", [128, 40], F32)
        self.epsc = A.alloc("epsc", [128, 1], F32)
        self.zeros = A.alloc("zeros", [128, 512], F32)
        self.iota1 = A.alloc("iota1", [128, S5T], F32)
        xT = self.xT
        io = A.alloc("io", [128, 128], I32)
        op("pool", lambda e: e.iota(io[:, :], pattern=[[1, 128]], base=0, channel_multiplier=-1), w=["io"])
        op("dve", lambda e: e.tensor_scalar(out=self.ident[:, :], in0=io[:, :], scalar1=0.0, scalar2=None,
                                             op0=ALU.is_equal), r=["io"], w=["ident"])
        op("pool", lambda e: e.iota(io[:, :], pattern=[[1, 128]], base=1, channel_multiplier=0), r=["ident"], w=["io"])
        op("dve", lambda e: e.tensor_copy(out=self.iota1[:, :], in_=io[:, 0:S5T]), r=["io"], w=["iota1"])
        op("dve", lambda e: e.memset(self.ones[:, :], 1.0), w=["ones"])
        op("dve", lambda e: e.memset(self.epsc[:, :], EPS), w=["epsc"])
        op("dve", lambda e: e.memset(self.zeros[:, :], 0.0), w=["zeros"])
        op("sp", lambda e: e.dma_start(out=self.cm[:, :], in_=cmask[:, :]), w=["cm"], dma=True)
        self.pmark = A.mark()

        if KD < 1:
            return self.finish()
        vin = A.alloc("vin", [42, D], F32)
        op("sp", lambda e: e.dma_start(out=vin[:, :], in_=vecs[:, :]), w=["vin"], dma=True)
        for dt in range(8):
            b = self.bank()
            bk = self.banks[b]
            op("pe", lambda e, dt=dt, bk=bk: e.transpose(bk[:, 0:42], vin[:, dt * 128:(dt + 1) * 128], self.ident[0:42, 0:42]),
               r=["vin", "ident"], w=[("ps", b)])
            op("act", lambda e, dt=dt, bk=bk: e.copy(out=self.vT[:, dt, :], in_=bk[:, 0:42]), r=[("ps", b)], w=["vT"])
        for i in range(4):
            for s in (0, 3):
                c = 16 + i * 4 + s
                op("dve", lambda e, c=c: e.tensor_scalar(out=self.vT[:, :, c:c + 1], in0=self.vT[:, :, c:c + 1], scalar1=0.5,
                                                         scalar2=None, op0=ALU.mult), r=["vT"], w=["vT"])
        if KD < 2:
            return self.finish()
        xin = [A.alloc("xin%d" % i, [128, D], F32) for i in range(2)]
        blocks = [(xp, tb * 128, 128, tb * 128) for tb in range(16)] + [(xs, 0, 32, NTP)]
        for bi, (src, r0, nr, t0) in enumerate(blocks):
            xb = xin[bi % 2]
            kx = ("xin", bi % 2)
            op("sp", lambda e, xb=xb, src=src, r0=r0, nr=nr: e.dma_start(out=xb[0:nr, :], in_=src[r0:r0 + nr, :]), w=[kx], dma=True)
            for hh in range(2):
                b = self.bank()
                bk = self.banks[b]
                for d4 in range(4):
                    dt = hh * 4 + d4
                    op("pe", lambda e, xb=xb, bk=bk, d4=d4, dt=dt, nr=nr: e.transpose(
                        bk[:, d4 * 128:d4 * 128 + nr], xb[0:nr, dt * 128:(dt + 1) * 128], self.ident[0:nr, 0:nr]),
                       r=[kx, "ident"], w=[("ps", b)])
                op("act" if hh == 0 else "dve", lambda e, bk=bk, hh=hh, t0=t0, nr=nr: (e.copy if hh == 0 else e.tensor_copy)(
                    out=xT[:, hh * 4:hh * 4 + 4, t0:t0 + nr],
                    in_=bk[:, :].rearrange("p (a b) -> p a b", a=4)[:, :, 0:nr]),
                   r=[("ps", b)], w=[("xT", t0 // 512)])
        P.barrier()
        A.reset(self.pmark)

        if KD < 3:
            return self.finish()
        plan = self.stop_after
        if plan is None:
            plan = []
            for i in range(4):
                plan += [("ffn", i, 0), ("s5", i) if i % 2 == 0 else ("pool", i), ("xattn", i), ("ffn", i, 3)]
        for st in plan:
            i = st[1]
            j = i // 2
            if st[0] == "ffn":
                s_ = st[2]
                self.ffn(i, s_, wg[i, s_ // 3], wu[i, s_ // 3], wd[i, s_ // 3])
            elif st[0] == "s5":
                self.s5(i, j, s5win[j], s5are[j], s5aim[j], s5bre[j], s5bim[j], s5cre[j], s5cim[j], s5ldt[j], s5glu[j],
                        sre[j], sim[j], o_hre[j], o_him[j], o_shre[j], o_shim[j])
            elif st[0] == "pool":
                self.pool(i, j, pwin[j], pwgrp[j], spool[j], o_pool[j], o_spool[j])
            elif st[0] == "xattn":
                self.xattn(i, wq[i], wk[i], wv[i], wo[i], memp, ck[i], cv[i], mk[i], mv[i])

        P.barrier()
        A.reset(self.pmark)
        xo = [A.alloc("xo%d" % i, [128, D], F32) for i in range(2)]
        blocks = [(yp, tb * 128, 128, tb * 128) for tb in range(16)] + [(ys, 0, 32, NTP)]
        for bi, (dst, r0, nr, t0) in enumerate(blocks):
            xb = xo[bi % 2]
            kx = ("xo", bi % 2)
            for hh in range(2):
                b = self.bank()
                bk = self.banks[b]
                for d4 in range(4):
                    dt = hh * 4 + d4
                    op("pe", lambda e, bk=bk, d4=d4, dt=dt, nr=nr, t0=t0: e.transpose(
                        bk[0:nr, d4 * 128:(d4 + 1) * 128], xT[:, dt, t0:t0 + nr], self.ident[:, :]),
                       r=[("xT", t0 // 512), "ident"], w=[("ps", b)])
                op("act" if hh == 0 else "dve", lambda e, bk=bk, hh=hh, nr=nr, xb=xb: (e.copy if hh == 0 else e.tensor_copy)(
                    out=xb[0:nr, hh * 512:(hh + 1) * 512], in_=bk[0:nr, :]), r=[("ps", b)], w=[kx])
            self.store(dst[r0:r0 + nr, :], xb[0:nr, :], [kx])
        return self.finish()

    def finish(self):
        self.P.op("sp", lambda e: e.nop(), r=self.out_keys)
        self.P.emit()
        return self.nc

    def vcol(self, c, dt):
        return self.vT[:, dt, c:c + 1]

    def rms_stats(self, src, W, srckeys, tmp, name):
        P = self.P
        sq, rstd = tmp
        b = self.bank()
        bk = self.banks[b]
        for dt in range(8):
            s = sq[dt % 2]
            ks = ("sq", name, dt % 2)
            P.op("act", lambda e, s=s, dt=dt: e.activation(out=s[:, 0:W], in_=src(dt), func=AF.Square), r=srckeys, w=[ks])
            P.op("pe", lambda e, s=s, dt=dt, bk=bk: e.matmul(bk[:, 0:W], self.ones[:, :], s[:, 0:W], start=(dt == 0), stop=(dt == 7)),
                 r=[ks, "ones"], w=[("ps", b)])
        kr = "rstd"
        P.op("act", lambda e, bk=bk: e.activation(out=rstd[:, 0:W], in_=bk[:, 0:W], func=AF.Sqrt, bias=self.epsc[:, 0:1], scale=1.0 / D),
             r=[("ps", b), "epsc"], w=[kr])
        P.op("dve", lambda e: e.reciprocal(out=rstd[:, 0:W], in_=rstd[:, 0:W]), r=[kr], w=[kr])
        return kr

    def norm_to(self, dst, dstkeys, t0, W, gcol, tmp, tidx):
        P = self.P
        xT = self.xT
        kr = self.rms_stats(lambda dt: xT[:, dt, t0:t0 + W], W, [("xT", tidx)], tmp, "n")
        rstd = tmp[1]
        for dt in range(8):
            P.op("dve", lambda e, dt=dt: e.scalar_tensor_tensor(out=dst(dt), in0=xT[:, dt, t0:t0 + W], scalar=self.vcol(gcol, dt),
                                                               in1=rstd[:, 0:W], op0=ALU.mult, op1=ALU.mult),
                 r=[("xT", tidx), kr, "vT"], w=dstkeys)

    def resid_update(self, f, fkeys, t0, W, gcol, tmp, tidx):
        P = self.P
        xT = self.xT
        kr = self.rms_stats(lambda dt: f[:, dt, 0:W], W, fkeys, tmp, "r")
        rstd = tmp[1]
        for dt in range(8):
            P.op("dve", lambda e, dt=dt: e.scalar_tensor_tensor(out=f[:, dt, 0:W], in0=f[:, dt, 0:W], scalar=self.vcol(gcol, dt),
                                                               in1=rstd[:, 0:W], op0=ALU.mult, op1=ALU.mult),
                 r=fkeys + [kr, "vT"], w=fkeys)
            P.op("pool", lambda e, dt=dt: e.tensor_tensor(out=xT[:, dt, t0:t0 + W], in0=xT[:, dt, t0:t0 + W], in1=f[:, dt, 0:W], op=ALU.add),
                 r=fkeys, w=[("xT", tidx)])

    def dual_linear(self, Wa, Wb, M, src, srckeys, tiles, consumer, wbufs, name, CB=512, nb=6, b0=0):
        P = self.P
        nblk = (M + CB - 1) // CB
        Wav = Wa.rearrange("(kc p) m -> p kc m", p=128)
        Wbv = Wb.rearrange("(kc p) m -> p kc m", p=128) if Wb is not None else None
        for blk in range(nblk):
            c0 = blk * CB
            cw = min(CB, M - c0)
            slot = blk % len(wbufs)
            wa, wb_ = wbufs[slot]
            ka = ("w", name, slot, 0)
            kb = ("w", name, slot, 1)
            P.op("pool", lambda e, wa=wa, c0=c0, cw=cw: e.dma_start(out=wa[:, :, 0:cw], in_=Wav[:, :, c0:c0 + cw]), w=[ka], dma=True)
            if Wb is not None:
                P.op("pool", lambda e, wb_=wb_, c0=c0, cw=cw: e.dma_start(out=wb_[:, :, 0:cw], in_=Wbv[:, :, c0:c0 + cw]), w=[kb], dma=True)
            for ml in range(cw // 128):
                mt = c0 // 128 + ml
                for ti, (so, W) in enumerate(tiles):
                    ba = b0 + self.bank(nb)
                    bka = self.banks[ba]
                    for kc in range(8):
                        P.op("pe", lambda e, wa=wa, ml=ml, kc=kc, so=so, W=W, bka=bka: e.matmul(
                            bka[:, 0:W], wa[:, kc, ml * 128:(ml + 1) * 128], src[:, kc, so:so + W], start=(kc == 0), stop=(kc == 7)),
                             r=[ka] + srckeys, w=[("ps", ba)])
                    bb = None
                    if Wb is not None:
                        bb = b0 + self.bank(nb)
                        bkb = self.banks[bb]
                        for kc in range(8):
                            P.op("pe", lambda e, wb_=wb_, ml=ml, kc=kc, so=so, W=W, bkb=bkb: e.matmul(
                                bkb[:, 0:W], wb_[:, kc, ml * 128:(ml + 1) * 128], src[:, kc, so:so + W], start=(kc == 0), stop=(kc == 7)),
                                 r=[kb] + srckeys, w=[("ps", bb)])
                    consumer(mt, ti, ba, bb)

    def ffn(self, i, s, Wg, Wu, Wd):
        nc, P, A = self.nc, self.P, self.A
        P.barrier()
        A.reset(self.pmark)
        act = A.alloc("act", [128, 22, 1056], BF16)
        fo = A.alloc("fo", [128, 8, 512], F32)
        sq = [A.alloc("sq%d" % k, [128, 512], BF16) for k in range(2)]
        rstd = A.alloc("rstd", [128, 512], F32)
        tmp = (sq, rstd)
        sg = [A.alloc("sg%d" % k, [128, 512], F32) for k in range(2)]
        wmark = A.mark()
        wbufs = [(A.alloc("wga%d" % k, [128, 8, 512], BF16), A.alloc("wgb%d" % k, [128, 8, 512], BF16)) for k in range(2)]
        hb = A.alloc("hb", [128, 8, 1056], BF16)
        A.reset(wmark)
        wdb = A.alloc("wdb", [128, 22, 1024], BF16)
        Wdv = Wd.rearrange("(f p) m -> p f m", p=128)
        gpre = i * 4 + s
        gpost = 16 + i * 4 + s
        for hi, half in enumerate(HALVES):
            tl = []
            off = 0
            for ti in half:
                t0, W = TILES[ti]
                tl.append((off, W, t0, ti))
                off += W
            for (so, W, t0, ti) in tl:
                self.norm_to(lambda dt, so=so, W=W: hb[:, dt, so:so + W], [("hb", ti), "wreg_h"], t0, W, gpre, tmp, ti)
            if KD == 10:
                continue
            sgi = [0]

            def consumer(mt, k, ba, bb, tl=tl):
                so, W, t0, ti = tl[k]
                sgt = sg[sgi[0] % 2]
                ks = ("sg", sgi[0] % 2)
                sgi[0] += 1
                bka, bkb = self.banks[ba], self.banks[bb]
                P.op("act", lambda e: e.activation(out=sgt[:, 0:W], in_=bka[:, 0:W], func=AF.Silu), r=[("ps", ba)], w=[ks])
                P.op("dve", lambda e: e.tensor_tensor(out=act[:, mt, so:so + W], in0=sgt[:, 0:W], in1=bkb[:, 0:W], op=ALU.mult),
                     r=[ks, ("ps", bb)], w=[("act", ti)])

            self.dual_linear(Wg, Wu, DFF, hb, [("hb", t[3]) for t in tl], [(t[0], t[1]) for t in tl], consumer,
                             wbufs, "wreg", CB=512)
            if KD == 11:
                continue
            wkeys = [("w", "wreg", sl, ab) for sl in range(2) for ab in range(2)] + [("hb", t[3]) for t in tl] + ["wreg_h"]
            for f in range(22):
                P.op("pool", lambda e, f=f: e.dma_start(out=wdb[:, f, :], in_=Wdv[:, f, :]), r=[],
                     w=[("wd", f)] + (wkeys if f == 0 else []), dma=True)
            if KD == 12:
                continue
            for (so, W, t0, ti) in tl:
                for f in range(22):
                    for dt in range(8):
                        bk = self.banks[dt]
                        P.op("pe", lambda e, f=f, dt=dt, bk=bk, so=so, W=W: e.matmul(
                            bk[:, 0:W], wdb[:, f, dt * 128:(dt + 1) * 128], act[:, f, so:so + W], start=(f == 0), stop=(f == 21)),
                             r=[("wd", f), ("wd", 0), ("act", ti)] + wkeys, w=[("ps", dt)])
                for dt in range(8):
                    bk = self.banks[dt]
                    P.op("act" if dt % 2 == 0 else "dve", lambda e, dt=dt, bk=bk, W=W: (e.copy if dt % 2 == 0 else e.tensor_copy)(
                        out=fo[:, dt, 0:W], in_=bk[:, 0:W]), r=[("ps", dt)], w=["fo"])
                self.resid_update(fo, ["fo"], t0, W, gpost, tmp, ti)

    def xattn(self, i, Wq, Wk, Wv, Wo, memp, ck, cv, mk, mv):
        nc, P, A = self.nc, self.P, self.A
        P.barrier()
        A.reset(self.pmark)
        sq = [A.alloc("sq%d" % k, [128, 512], BF16) for k in range(2)]
        rstd = A.alloc("rstd", [128, 512], F32)
        tmp = (sq, rstd)
        KT = [A.alloc("KT%d" % g, [128, 8, 256], BF16) for g in range(2)]
        Vt = [A.alloc("V%d" % g, [128, 2, D], BF16) for g in range(2)]
        mT = A.alloc("mT", [128, 8, 256], F32)
        mn = A.alloc("mn", [128, 8, 256], BF16)
        mrow = A.alloc("mrow", [128, 2, D], F32)
        kvo = [A.alloc("kvo%d" % k, [128, 512], F32) for k in range(2)]
        wbufs = [(A.alloc("wa%d" % k, [128, 8, 512], BF16), None) for k in range(2)]
        hN = A.alloc("hN", [128, 8, 512], BF16)
        qT = A.alloc("qT", [128, 8, 512], BF16)
        oT = A.alloc("oT", [128, 8, 512], BF16)
        Et = [A.alloc("E%d" % k, [128, 512], BF16) for k in range(4)]
        den = A.alloc("den", [128, 512], F32)
        ao = A.alloc("ao", [128, 8, 512], F32)
        op = P.op
        op("sp", lambda e: e.dma_start(out=mrow[:, :, :], in_=memp.rearrange("(mb p) d -> p mb d", p=128)), w=["mrow"], dma=True)
        for mb in range(2):
            for hh in range(2):
                b = self.bank()
                bk = self.banks[b]
                for d4 in range(4):
                    dt = hh * 4 + d4
                    op("pe", lambda e, bk=bk, d4=d4, dt=dt, mb=mb: e.transpose(bk[:, d4 * 128:(d4 + 1) * 128], mrow[:, mb, dt * 128:(dt + 1) * 128],
                                                                              self.ident[:, :]), r=["mrow", "ident"], w=[("ps", b)])
                op("act", lambda e, bk=bk, hh=hh, mb=mb: e.copy(out=mT[:, hh * 4:hh * 4 + 4, mb * 128:(mb + 1) * 128],
                                                                in_=bk[:, :].rearrange("p (a b) -> p a b", a=4)), r=[("ps", b)], w=["mT"])
        kr = self.rms_stats(lambda dt: mT[:, dt, :], 256, ["mT"], tmp, "m")
        for dt in range(8):
            op("dve", lambda e, dt=dt: e.scalar_tensor_tensor(out=mn[:, dt, :], in0=mT[:, dt, :], scalar=self.vcol(32 + i, dt),
                                                              in1=rstd[:, 0:256], op0=ALU.mult, op1=ALU.mult), r=["mT", kr, "vT"], w=["mn"])

        def cons_kt(mt, k, ba, bb):
            bk = self.banks[ba]
            op("act", lambda e: e.copy(out=KT[0][:, mt, :], in_=bk[:, 0:256]), r=[("ps", ba)], w=["KT0"])
        self.dual_linear(Wk, None, D, mn, ["mn"], [(0, 256)], cons_kt, wbufs, "xw")
        kvi = [0]
        for which, Wm, dst in ((0, Wk, mk), (1, Wv, mv)):
            Wmv = Wm.rearrange("(kc p) m -> p kc m", p=128)
            for eb in range(2):
                slot = (kvi[0]) % 2
                kvi[0] += 1
                wa = wbufs[slot][0]
                ka = ("w", "xw", slot, 0)
                op("pool", lambda e, wa=wa, eb=eb, Wmv=Wmv: e.dma_start(out=wa[:, :, :], in_=Wmv[:, :, eb * 512:(eb + 1) * 512]), w=[ka], dma=True)
                for mb in range(2):
                    b = self.bank()
                    bk = self.banks[b]
                    for kc in range(8):
                        op("pe", lambda e, wa=wa, kc=kc, mb=mb, bk=bk: e.matmul(bk[:, :], mn[:, kc, mb * 128:(mb + 1) * 128], wa[:, kc, :],
                                                                                start=(kc == 0), stop=(kc == 7)), r=[ka, "mn"], w=[("ps", b)])
                    ko = kvo[(mb + eb) % 2]
                    kk = ("kvo", (mb + eb) % 2)
                    op("act", lambda e, ko=ko, bk=bk: e.copy(out=ko[:, :], in_=bk[:, :]), r=[("ps", b)], w=[kk])
                    if which == 1:
                        op("dve", lambda e, ko=ko, mb=mb, eb=eb: e.tensor_copy(out=Vt[0][:, mb, eb * 512:(eb + 1) * 512], in_=ko[:, :]),
                           r=[kk], w=["V0"])
                    self.store(dst[mb * 128:(mb + 1) * 128, eb * 512:(eb + 1) * 512], ko[:, :], [kk])
        if KD == 20:
            return
        op("pool", lambda e: e.dma_start(out=Vt[1][:, :, :], in_=cv.rearrange("(mb p) d -> p mb d", p=128)), w=["V1"], dma=True)
        op("sp", lambda e: e.dma_start(out=mrow[:, :, :], in_=ck.rearrange("(mb p) d -> p mb d", p=128)), w=["mrow"], dma=True)
        for mb in range(2):
            for hh in range(2):
                b = self.bank()
                bk = self.banks[b]
                for d4 in range(4):
                    dt = hh * 4 + d4
                    op("pe", lambda e, bk=bk, d4=d4, dt=dt, mb=mb: e.transpose(bk[:, d4 * 128:(d4 + 1) * 128], mrow[:, mb, dt * 128:(dt + 1) * 128],
                                                                              self.ident[:, :]), r=["mrow", "ident"], w=[("ps", b)])
                op("act", lambda e, bk=bk, hh=hh, mb=mb: e.copy(out=KT[1][:, hh * 4:hh * 4 + 4, mb * 128:(mb + 1) * 128],
                                                                in_=bk[:, :].rearrange("p (a b) -> p a b", a=4)), r=[("ps", b)], w=["KT1"])
        if KD == 21:
            return
        def do_tile(ti, t0, W):
            g = 1 if ti == 4 else 0
            kKT, kV = "KT%d" % g, "V%d" % g
            self.norm_to(lambda dt: hN[:, dt, 0:W], ["hN"], t0, W, i * 4 + 2, tmp, ti)

            def cons_q(mt, k, ba, bb, W=W):
                bk = self.banks[ba]
                op("act", lambda e: e.activation(out=qT[:, mt, 0:W], in_=bk[:, 0:W], func=AF.Copy, scale=1.0 / 16.0), r=[("ps", ba)], w=["qT"])
            self.dual_linear(Wq, None, D, hN, ["hN"], [(0, W)], cons_q, wbufs, "xw")
            if KD == 22:
                return
            for h in range(4):
                bd = self.bank()
                bkd = self.banks[bd]
                for mb in range(2):
                    b = self.bank()
                    bk = self.banks[b]
                    for eh in range(2):
                        et = h * 2 + eh
                        op("pe", lambda e, bk=bk, et=et, mb=mb, eh=eh: e.matmul(bk[:, 0:W], KT[g][:, et, mb * 128:(mb + 1) * 128], qT[:, et, 0:W],
                                                                                start=(eh == 0), stop=(eh == 1)), r=[kKT, "qT"], w=[("ps", b)])
                    E = Et[(h * 2 + mb) % 4]
                    kE = ("E", (h * 2 + mb) % 4)
                    op("act", lambda e, E=E, bk=bk: e.activation(out=E[:, 0:W], in_=bk[:, 0:W], func=AF.Exp), r=[("ps", b)], w=[kE])
                    op("pe", lambda e, E=E, bkd=bkd, mb=mb: e.matmul(bkd[:, 0:W], self.ones[:, :], E[:, 0:W], start=(mb == 0), stop=(mb == 1)),
                       r=[kE, "ones"], w=[("ps", bd)])
                op("dve", lambda e, bkd=bkd: e.reciprocal(out=den[:, 0:W], in_=bkd[:, 0:W]), r=[("ps", bd)], w=["den"])
                for eh in range(2):
                    et = h * 2 + eh
                    b = self.bank()
                    bk = self.banks[b]
                    for mb in range(2):
                        E = Et[(h * 2 + mb) % 4]
                        kE = ("E", (h * 2 + mb) % 4)
                        op("pe", lambda e, bk=bk, E=E, mb=mb, et=et: e.matmul(bk[:, 0:W], Vt[g][:, mb, et * 128:(et + 1) * 128], E[:, 0:W],
                                                                              start=(mb == 0), stop=(mb == 1)), r=[kE, kV], w=[("ps", b)])
                    op("dve", lambda e, bk=bk, et=et: e.tensor_tensor(out=oT[:, et, 0:W], in0=bk[:, 0:W], in1=den[:, 0:W], op=ALU.mult),
                       r=[("ps", b), "den"], w=["oT"])

            if KD == 23:
                return
            def cons_o(mt, k, ba, bb, W=W):
                bk = self.banks[ba]
                op("act", lambda e: e.copy(out=ao[:, mt, 0:W], in_=bk[:, 0:W]), r=[("ps", ba)], w=["ao"])
            self.dual_linear(Wo, None, D, oT, ["oT"], [(0, W)], cons_o, wbufs, "xw")
            self.resid_update(ao, ["ao"], t0, W, 16 + i * 4 + 2, tmp, ti)
        for ti, (t0, W) in enumerate(TILES):
            do_tile(ti, t0, W)

    def exchange(self, src_ap, ncols, dst):
        P = self.P
        g_in, g_out = self.g_in, self.g_out
        P.op("sp", lambda e: e.dma_start(out=g_in[:, 0:ncols], in_=src_ap), r=["xsrc"], w=["g_in"], dma=True)
        P.op("pool", lambda e: e.collective_compute("AllGather", ALU.bypass, replica_groups=[list(range(8))],
                                                    ins=[g_in.ap().opt()], outs=[g_out.ap().opt()]), r=["g_in"], w=["g_out"], dma="cc")
        P.op("sp", lambda e: e.dma_start(out=dst[:, :, 0:ncols], in_=g_out.ap().rearrange("(r p) n -> p r n", p=128)[:, :, 0:ncols]),
             r=["g_out"], w=["xdst"], dma=True)

    def pool(self, i, j, Win, Wgrp, hist_s, o_pool, o_spool):
        nc, P, A = self.nc, self.P, self.A
        op = P.op
        P.barrier()
        A.reset(self.pmark)
        sq = [A.alloc("sq%d" % k, [128, 512], BF16) for k in range(2)]
        rstd = A.alloc("rstd", [128, 512], F32)
        tmp = (sq, rstd)
        wbufs = [(A.alloc("wa%d" % k, [128, 8, 512], BF16), None) for k in range(2)]
        hN = A.alloc("hN", [128, 8, 512], BF16)
        ue = A.alloc("ue", [128, 8, 15 + NTP], BF16)
        ues = A.alloc("ues", [128, 8, 15 + NTS], BF16)
        tail = A.alloc("tail", [128, 8, 15], F32)
        tails = A.alloc("tails", [128, 8, 15], F32)
        gath = A.alloc("gath", [128, 8, 120], F32)
        hist = A.alloc("hist", [128, 120], F32)
        hrow = A.alloc("hrow", [15, D], F32)
        trow = A.alloc("trow", [15, D], F32)
        s1 = A.alloc("s1", [128, 15 + NTP], F32)
        s2 = A.alloc("s2", [128, 15 + NTP], F32)
        pl = A.alloc("pl", [128, 8, NT], BF16)
        wgb = A.alloc("wgb", [128, 8, 256], BF16)
        icn = A.alloc("icn", [128, 4, 16], F32)
        mix = nc.alloc_sbuf_tensor_at("mixp_%d" % i, [128, 8, 512], F32, offset=self._off(wbufs[0][0]))
        for ti, (t0, W) in enumerate(TILES):
            self.norm_to(lambda dt, W=W: hN[:, dt, 0:W], ["hN"], t0, W, i * 4 + 1, tmp, ti)

            def cons_u(mt, k, ba, bb, W=W, t0=t0, ti=ti):
                bk = self.banks[ba]
                if ti < 4:
                    op("act", lambda e: e.copy(out=ue[:, mt, 15 + t0:15 + t0 + W], in_=bk[:, 0:W]), r=[("ps", ba)], w=[("ue", ti), ("psr", ba)])
                    if ti == 3:
                        op("dve", lambda e: e.tensor_copy(out=tail[:, mt, :], in_=bk[:, W - 15:W]), r=[("ps", ba)], w=["tail", ("psr", ba)])
                else:
                    op("act", lambda e: e.copy(out=ues[:, mt, 15:15 + W], in_=bk[:, 0:W]), r=[("ps", ba)], w=["ues", ("psr", ba)])
                    op("dve", lambda e: e.tensor_copy(out=tails[:, mt, :], in_=bk[:, W - 15:W]), r=[("ps", ba)], w=["tails", ("psr", ba)])
            self.dual_linear(Win, None, D, hN, ["hN"], [(0, W)], cons_u, wbufs, "pw")
        for (tl_, dst, kk) in ((tail, o_pool, "tail"), (tails, o_spool, "tails")):
            for hh in range(2):
                b = self.bank()
                bk = self.banks[b]
                for d4 in range(4):
                    dt = hh * 4 + d4
                    op("pe", lambda e, bk=bk, d4=d4, dt=dt, tl_=tl_: e.transpose(bk[0:15, d4 * 128:(d4 + 1) * 128], tl_[:, dt, :], self.ident[:, :]),
                       r=[kk, "ident"], w=[("ps", b)])
                op("act", lambda e, bk=bk, hh=hh: e.copy(out=trow[:, hh * 512:(hh + 1) * 512], in_=bk[0:15, :]), r=[("ps", b)], w=["trow"])
            self.store(dst[:, :], trow[:, :], ["trow"])
        op("dve", lambda e: e.tensor_copy(out=hist[:, :], in_=tail[:, :, :].rearrange("p a b -> p (a b)")), r=["tail"], w=["xsrc"])
        self.exchange(hist[:, :], 120, gath)
        for r in range(8):
            if r == 0:
                op("dve", lambda e: e.tensor_scalar(out=hist[:, :], in0=gath[:, 0, :], scalar1=self.cm[:, 24:25], scalar2=None, op0=ALU.mult),
                   r=["xdst", "cm"], w=["hist"])
            else:
                op("dve", lambda e, r=r: e.scalar_tensor_tensor(out=hist[:, :], in0=gath[:, r, :], scalar=self.cm[:, 24 + r:25 + r], in1=hist[:, :],
                                                                op0=ALU.mult, op1=ALU.add), r=["xdst", "cm", "hist"], w=["hist"])
        op("dve", lambda e: e.tensor_copy(out=ue[:, :, 0:15], in_=hist[:, :].rearrange("p (a b) -> p a b", a=8)), r=["hist"], w=[("ue", 0)])
        op("sp", lambda e: e.dma_start(out=hrow[:, :], in_=hist_s[:, :]), w=["hrow"], dma=True)
        for hh in range(2):
            b = self.bank()
            bk = self.banks[b]
            for d4 in range(4):
                dt = hh * 4 + d4
                op("pe", lambda e, bk=bk, d4=d4, dt=dt: e.transpose(bk[:, d4 * 128:d4 * 128 + 15], hrow[:, dt * 128:(dt + 1) * 128], self.ident[0:15, 0:15]),
                   r=["hrow", "ident"], w=[("ps", b)])
            op("act", lambda e, bk=bk, hh=hh: e.copy(out=ues[:, hh * 4:hh * 4 + 4, 0:15],
                                                    in_=bk[:, :].rearrange("p (a b) -> p a b", a=4)[:, :, 0:15]), r=[("ps", b)], w=["ues"])
        for g in range(4):
            w_ = float(2 ** (g + 1))
            op("dve", lambda e, g=g, w_=w_: e.tensor_scalar(out=icn[:, g, :], in0=self.iota1[:, 0:16], scalar1=self.cm[:, 32:33], scalar2=w_,
                                                            op0=ALU.add, op1=ALU.min), r=["iota1", "cm"], w=["icn"])
        op("dve", lambda e: e.reciprocal(out=icn[:, :, :], in_=icn[:, :, :]), r=["icn"], w=["icn"])
        allue = [("ue", t) for t in range(4)]
        for (U, n, toff, keys, samp) in ((ue, NTP, 0, allue, False), (ues, NTS, NTP, ["ues"], True)):
            L = 15 + n
            for dt in range(8):
                g = dt // 2
                eng = "dve" if dt % 2 == 0 else "pool"
                op(eng, lambda e, U=U, dt=dt, L=L: e.tensor_tensor(out=s1[:, 1:L], in0=U[:, dt, 1:L], in1=U[:, dt, 0:L - 1], op=ALU.add),
                   r=keys, w=["s1"])
                cur, ck_, oth, ok_ = s1, "s1", s2, "s2"
                sh = 2
                for lv in range(g):
                    op(eng, lambda e, cur=cur, oth=oth, sh=sh, L=L: e.tensor_tensor(out=oth[:, 2 * sh - 1:L], in0=cur[:, 2 * sh - 1:L],
                                                                                    in1=cur[:, sh - 1:L - sh], op=ALU.add), r=[ck_], w=[ok_])
                    cur, ck_, oth, ok_ = oth, ok_, cur, ck_
                    sh *= 2
                w_ = float(2 ** (g + 1))
                op("dve", lambda e, cur=cur, U=U, dt=dt, n=n, toff=toff, w_=w_: e.scalar_tensor_tensor(
                    out=pl[:, dt, toff:toff + n], in0=cur[:, 15:15 + n], scalar=1.0 / w_, in1=U[:, dt, 15:15 + n], op0=ALU.mult, op1=ALU.subtract),
                   r=[ck_] + keys, w=["pl"])
                if not samp:
                    op("dve", lambda e, cur=cur, g=g: e.tensor_tensor(out=s2[:, 0:15] if cur is s1 else s1[:, 0:15], in0=cur[:, 15:30],
                                                                       in1=icn[:, g, 0:15], op=ALU.mult), r=[ck_, "icn"], w=[ok_])
                    o2 = s2 if cur is s1 else s1
                    op("dve", lambda e, o2=o2, dt=dt: e.tensor_tensor(out=pl[:, dt, 0:15], in0=o2[:, 0:15], in1=ue[:, dt, 15:30], op=ALU.subtract),
                       r=[ok_] + keys, w=["pl"])
        P.barrier()
        op("pool", lambda e: e.dma_start(out=wgb[:, :, :], in_=Wgrp.rearrange("g (kc p) o -> p (g kc) o", p=128)), w=["wgb"], dma=True)
        def do_tile(ti, t0, W):
            for dt in range(8):
                g, mo = dt // 2, dt % 2
                b = self.bank()
                bk = self.banks[b]
                for kc in range(2):
                    op("pe", lambda e, bk=bk, g=g, mo=mo, kc=kc: e.matmul(bk[:, 0:W], wgb[:, g * 2 + kc, mo * 128:(mo + 1) * 128], pl[:, g * 2 + kc, t0:t0 + W],
                                                                          start=(kc == 0), stop=(kc == 1)), r=["wgb", "pl"], w=[("ps", b)])
                op("dve", lambda e, bk=bk, dt=dt: e.tensor_scalar(out=mix[:, dt, 0:W], in0=bk[:, 0:W], scalar1=self.vcol(38 + j, dt),
                                                                  scalar2=self.vcol(40 + j, dt), op0=ALU.add, op1=ALU.mult),
                   r=[("ps", b), "vT"], w=["mix"])
            self.resid_update(mix, ["mix"], t0, W, 16 + i * 4 + 1, tmp, ti)
        for ti, (t0, W) in enumerate(TILES):
            do_tile(ti, t0, W)

    def trig(self, v_ap, n, outc, outs, keys_in, kout, tmps):
        op = self.P.op
        v2, fi, ff, ii = tmps
        for (shift, out) in ((0.25, outc), (0.0, outs)):
            op("dve", lambda e, shift=shift: e.tensor_scalar(out=v2[:, 0:n], in0=v_ap, scalar1=shift, scalar2=None, op0=ALU.add),
               r=keys_in, w=["tg_v2"])
            op("dve", lambda e: e.tensor_copy(out=ii[:, 0:n], in_=v2[:, 0:n]), r=["tg_v2"], w=["tg_ii"])
            op("dve", lambda e: e.tensor_copy(out=fi[:, 0:n], in_=ii[:, 0:n]), r=["tg_ii"], w=["tg_fi"])
            op("dve", lambda e: e.tensor_tensor(out=ff[:, 0:n], in0=v2[:, 0:n], in1=fi[:, 0:n], op=ALU.subtract), r=["tg_v2", "tg_fi"], w=["tg_ff"])
            op("dve", lambda e: e.tensor_scalar(out=ff[:, 0:n], in0=ff[:, 0:n], scalar1=0.5, scalar2=-0.5, op0=ALU.min, op1=ALU.max),
               r=["tg_ff"], w=["tg_ff"])
            op("act", lambda e, out=out: e.activation(out=out, in_=ff[:, 0:n], func=AF.Sin, scale=TWO_PI), r=["tg_ff"], w=[kout])

    def cmul(self, ore, oim, are, aim, bre, bim, t1, keys_r, kw):
        op = self.P.op
        op("dve", lambda e: e.tensor_tensor(out=t1, in0=aim, in1=bim, op=ALU.mult), r=keys_r, w=["cm_t1"])
        op("dve", lambda e: e.tensor_tensor(out=ore, in0=are, in1=bre, op=ALU.mult), r=keys_r, w=[kw])
        op("dve", lambda e: e.tensor_tensor(out=ore, in0=ore, in1=t1, op=ALU.subtract), r=["cm_t1", kw], w=[kw])
        op("dve", lambda e: e.tensor_tensor(out=t1, in0=are, in1=bim, op=ALU.mult), r=keys_r + [kw], w=["cm_t1"])
        op("dve", lambda e: e.tensor_tensor(out=oim, in0=aim, in1=bre, op=ALU.mult), r=keys_r, w=[kw])
        op("dve", lambda e: e.tensor_tensor(out=oim, in0=oim, in1=t1, op=ALU.add), r=["cm_t1", kw], w=[kw])

    def s5(self, i, j, Win, Are, Aim, Bre, Bim, Cre, Cim, Ldt, Wglu, sre, sim, o_hre, o_him, o_shre, o_shim):
        nc, P, A = self.nc, self.P, self.A
        op = P.op
        P.barrier()
        A.reset(self.pmark)
        T = S5T
        sq = [A.alloc("sq%d" % k, [128, 256], BF16) for k in range(2)]
        rstd = A.alloc("rstd", [128, 256], F32)
        tmp = (sq, rstd)
        yT = A.alloc("yT", [128, 8, NT], BF16)
        rotc = A.alloc("rotc", [128, 32, T], BF16)
        rots = A.alloc("rots", [128, 32, T], BF16)
        Blhs = [A.alloc("Blhs%d" % k, [128, 32, 128], BF16) for k in range(2)]
        Clhs = [A.alloc("Clhs%d" % k, [128, 32, 128], BF16) for k in range(2)]
        Dg = A.alloc("Dg", [128, 8, 128], BF16)
        names = ("are", "aim", "dtv", "mag", "th", "lr", "li", "cre", "cim", "ncim", "t1", "t2", "t3", "cT", "sT", "c32", "s32",
                 "p1r", "p1i", "p2r", "p2i", "hir", "hii", "str", "sti", "ssr", "ssi", "wlr", "wli", "a1r", "a1i", "a2r", "a2i", "a3r", "a3i",
                 "v", "fr", "fim", "s2r", "s2i", "fnr", "fni", "lTr", "lTi", "nhi", "hnr", "hni")
        sm = {n_: A.alloc("sm_" + n_, [128, 32], F32) for n_ in names}
        smi = A.alloc("smi", [128, 32], I32)
        st64 = A.alloc("st64", [128, 64], F32)
        arow = A.alloc("arow", [32, 128], F32)
        orow = A.alloc("orow", [32, 128], F32)
        SW = 256
        wbufs = [(A.alloc("wa%d" % k, [128, 8, 128], BF16), A.alloc("wb%d" % k, [128, 8, 128], BF16)) for k in range(2)]
        hN = A.alloc("hN", [128, 8, SW], BF16)
        dd = {n_: [A.alloc("%s%d" % (n_, k), [128, 4, T], BF16) for k in range(2)] for n_ in ("d1", "d2", "d3", "d4")}
        mix = nc.alloc_sbuf_tensor_at("mixs_%d" % i, [128, 8, SW], F32, offset=self._off(hN))
        wk_ = {n_: [A.alloc("%s%d" % (n_, k), [128, 4, T], F32) for k in range(2)] for n_ in ("bmr", "bmi", "wr", "wi")}
        tt_ = {n_: [A.alloc("%s%d" % (n_, k), [128, 4, T], BF16) for k in range(2)] for n_ in ("t1", "t2", "t3", "t4")}
        sgl = [nc.alloc_sbuf_tensor_at("sgl%d_%d" % (k, i), [128, SW], F32, offset=self._off(wk_["bmr"][0]) + k * 1024) for k in range(2)]
        rpw2 = [nc.alloc_sbuf_tensor_at("rpw%d_%d" % (k, i), [128, T], F32, offset=self._off(wk_["wr"][0]) + k * 512) for k in range(2)]
        yo = self._off(yT)
        gath = nc.alloc_sbuf_tensor_at("gath_%d" % i, [128, 8, 64], F32, offset=self._off(tt_["t1"][0]))
        S = lambda n_: sm[n_][:, :]

        def small_T(src_dram, dst, kname):
            op("sp", lambda e: e.dma_start(out=arow[:, :], in_=src_dram[:, :]), w=["arow"], dma=True)
            b = self.bank()
            bk = self.banks[b]
            op("pe", lambda e: e.transpose(bk[:, 0:32], arow[:, :], self.ident[0:32, 0:32]), r=["arow", "ident"], w=[("ps", b)])
            op("act", lambda e: e.copy(out=dst[:, :], in_=bk[:, 0:32]), r=[("ps", b)], w=[kname])

        def small_out(src, dst_dram, kname):
            b = self.bank()
            bk = self.banks[b]
            op("pe", lambda e: e.transpose(bk[0:32, 0:128], src[:, :], self.ident[:, :]), r=[kname, "ident"], w=[("ps", b)])
            op("act", lambda e: e.copy(out=orow[:, :], in_=bk[0:32, 0:128]), r=[("ps", b)], w=["orow"])
            self.store(dst_dram[:, :], orow[:, :], ["orow"])

        small_T(Are, sm["are"], "s_are")
        small_T(Aim, sm["aim"], "s_aim")
        small_T(sre, sm["ssr"], "ss")
        small_T(sim, sm["ssi"], "ss")
        for gl in range(2):
            op("sp", lambda e, gl=gl: e.dma_start(out=sm["dtv"][64 * gl:64 * gl + 64, :], in_=Ldt[gl:gl + 1, :].to_broadcast([64, 32])),
               w=["s_dtv"], dma=True)
        op("act", lambda e: e.activation(out=S("dtv"), in_=S("dtv"), func=AF.Exp), r=["s_dtv"], w=["s_dtv"])
        op("dve", lambda e: e.tensor_tensor(out=S("t1"), in0=S("are"), in1=S("dtv"), op=ALU.mult), r=["s_are", "s_dtv"], w=["s_t1"])
        op("act", lambda e: e.activation(out=S("mag"), in_=S("t1"), func=AF.Exp), r=["s_t1"], w=["s_mag"])
        op("dve", lambda e: e.tensor_tensor(out=S("th"), in0=S("aim"), in1=S("dtv"), op=ALU.mult), r=["s_aim", "s_dtv"], w=["s_th"])
        op("dve", lambda e: e.tensor_scalar(out=S("th"), in0=S("th"), scalar1=1.0 / TWO_PI, scalar2=None, op0=ALU.mult), r=["s_th"], w=["s_th"])
        tg = (sm["t2"], sm["t3"], sm["v"], smi)
        self.trig(S("th"), 32, S("fr"), S("fim"), ["s_th"], "s_f", tg)
        op("dve", lambda e: e.tensor_tensor(out=S("lr"), in0=S("fr"), in1=S("mag"), op=ALU.mult), r=["s_f", "s_mag"], w=["s_lr"])
        op("dve", lambda e: e.tensor_tensor(out=S("li"), in0=S("fim"), in1=S("mag"), op=ALU.mult), r=["s_f", "s_mag"], w=["s_li"])
        op("dve", lambda e: e.tensor_tensor(out=S("t1"), in0=S("are"), in1=S("are"), op=ALU.mult), r=["s_are", "s_mag"], w=["s_t1"])
        op("dve", lambda e: e.tensor_tensor(out=S("a1r"), in0=S("aim"), in1=S("aim"), op=ALU.mult), r=["s_aim"], w=["s_a1r"])
        op("dve", lambda e: e.tensor_tensor(out=S("t1"), in0=S("t1"), in1=S("a1r"), op=ALU.add), r=["s_t1", "s_a1r"], w=["s_t1"])
        op("dve", lambda e: e.reciprocal(out=S("t1"), in_=S("t1")), r=["s_t1"], w=["s_t1"])
        op("dve", lambda e: e.tensor_scalar(out=S("a1r"), in0=S("lr"), scalar1=-1.0, scalar2=None, op0=ALU.add), r=["s_lr", "s_t1"], w=["s_a1r"])
        op("dve", lambda e: e.tensor_scalar(out=S("a1i"), in0=S("aim"), scalar1=-1.0, scalar2=None, op0=ALU.mult), r=["s_aim"], w=["s_a1i"])
        self.cmul(S("cre"), S("cim"), S("a1r"), S("li"), S("are"), S("a1i"), S("a2r"), ["s_a1r", "s_li", "s_are", "s_a1i"], "s_c")
        op("dve", lambda e: e.tensor_tensor(out=S("cre"), in0=S("cre"), in1=S("t1"), op=ALU.mult), r=["s_c", "s_t1"], w=["s_c"])
        op("dve", lambda e: e.tensor_tensor(out=S("cim"), in0=S("cim"), in1=S("t1"), op=ALU.mult), r=["s_c", "s_t1"], w=["s_c"])
        op("dve", lambda e: e.tensor_scalar(out=S("ncim"), in0=S("cim"), scalar1=-1.0, scalar2=None, op0=ALU.mult), r=["s_c"], w=["s_c"])
        for (mult, oc, os_, kk) in ((float(T), "cT", "sT", "s_rT"), (32.0, "c32", "s32", "s_r32"), (2048.0, "p1r", "p1i", "s_p1"), (4096.0, "p2r", "p2i", "s_p2")):
            op("dve", lambda e, mult=mult: e.tensor_scalar(out=S("a2r"), in0=S("th"), scalar1=mult, scalar2=None, op0=ALU.mult), r=["s_th", "s_c"], w=["s_a2r"])
            self.trig(S("a2r"), 32, S(oc), S(os_), ["s_a2r"], kk, tg)
            if mult > 1000:
                op("dve", lambda e: e.tensor_tensor(out=S("a2i"), in0=S("are"), in1=S("dtv"), op=ALU.mult), r=["s_are", "s_dtv", kk], w=["s_a2i"])
                op("act", lambda e, mult=mult: e.activation(out=S("a2i"), in_=S("a2i"), func=AF.Exp, scale=mult), r=["s_a2i"], w=["s_a2i"])
                op("dve", lambda e, oc=oc: e.tensor_tensor(out=S(oc), in0=S(oc), in1=S("a2i"), op=ALU.mult), r=[kk, "s_a2i"], w=[kk])
                op("dve", lambda e, os_=os_: e.tensor_tensor(out=S(os_), in0=S(os_), in1=S("a2i"), op=ALU.mult), r=[kk, "s_a2i"], w=[kk])
        n8 = 8 * T
        tv = nc.alloc_sbuf_tensor_at("tv_%d" % i, [128, n8], F32, offset=yo)
        tfi = nc.alloc_sbuf_tensor_at("tfi_%d" % i, [128, n8], F32, offset=yo + 4 * n8)
        tff = nc.alloc_sbuf_tensor_at("tff_%d" % i, [128, n8], F32, offset=yo + 8 * n8)
        tii = nc.alloc_sbuf_tensor_at("tii_%d" % i, [128, n8], I32, offset=yo + 12 * n8)
        tva = nc.alloc_sbuf_tensor_at("tva_%d" % i, [128, n8], F32, offset=yo + 16 * n8)
        for ch in range(4):
            for pj in range(8):
                jj = ch * 8 + pj
                op("dve", lambda e, pj=pj, jj=jj: e.tensor_scalar(out=tva[:, pj * T:(pj + 1) * T], in0=self.iota1[:, 0:T], scalar1=sm["th"][:, jj:jj + 1],
                                                                  scalar2=None, op0=ALU.mult), r=["iota1", "s_th", "rot"], w=["tva"])
            self.trig(tva[:, :], n8, rotc[:, ch * 8:(ch + 1) * 8, :].rearrange("p a b -> p (a b)"),
                      rots[:, ch * 8:(ch + 1) * 8, :].rearrange("p a b -> p (a b)"), ["tva"], "rot", (tv, tfi, tff, tii))
        P.barrier()
        Bw = [nc.alloc_sbuf_tensor_at("Bw%d_%d" % (k, i), [128, 32, 128], F32, offset=yo + k * 16384) for k in range(2)]
        for k, Bsrc in enumerate((Bre, Bim)):
            op("dve" if k == 0 else "pool", lambda e, k=k: e.memset(Bw[k][:, :, :], 0.0), w=[("Bw", k)])
            Bv = Bsrc.rearrange("(j4 jm gl) p c -> gl jm p j4 c", jm=4, gl=2)
            for gl in range(2):
                for jm in range(4):
                    op("sp", lambda e, k=k, gl=gl, jm=jm, Bv=Bv: e.dma_start(
                        out=Bw[k][64 * gl:64 * gl + 64, :, :].rearrange("p (j4 jm) c -> p jm j4 c", jm=4)[:, jm, :, 32 * jm + 16 * gl:32 * jm + 16 * gl + 16],
                        in_=Bv[gl, jm]), w=[("Bw", k)], dma=True)
        Bt = [nc.alloc_sbuf_tensor_at("Bt%d_%d" % (k, i), [128, 128], F32, offset=self._off(hN) + k * 512) for k in range(4)]
        for jj in range(32):
            a, b2 = Bt[(jj % 2) * 2], Bt[(jj % 2) * 2 + 1]
            ka, kb = ("Bt", (jj % 2) * 2), ("Bt", (jj % 2) * 2 + 1)
            op("dve", lambda e, jj=jj, a=a: e.tensor_scalar(out=a[:, :], in0=Bw[0][:, jj, :], scalar1=sm["cre"][:, jj:jj + 1], scalar2=None, op0=ALU.mult),
               r=[("Bw", 0), "s_c"], w=[ka])
            op("dve", lambda e, jj=jj, a=a: e.scalar_tensor_tensor(out=a[:, :], in0=Bw[1][:, jj, :], scalar=sm["ncim"][:, jj:jj + 1], in1=a[:, :],
                                                                   op0=ALU.mult, op1=ALU.add), r=[("Bw", 1), "s_c", ka], w=[ka])
            op("dve", lambda e, jj=jj, b2=b2: e.tensor_scalar(out=b2[:, :], in0=Bw[1][:, jj, :], scalar1=sm["cre"][:, jj:jj + 1], scalar2=None, op0=ALU.mult),
               r=[("Bw", 1), "s_c"], w=[kb])
            op("dve", lambda e, jj=jj, b2=b2: e.scalar_tensor_tensor(out=b2[:, :], in0=Bw[0][:, jj, :], scalar=sm["cim"][:, jj:jj + 1], in1=b2[:, :],
                                                                     op0=ALU.mult, op1=ALU.add), r=[("Bw", 0), "s_c", kb], w=[kb])
            for k, (src, kk) in enumerate(((a, ka), (b2, kb))):
                b = self.bank()
                bk = self.banks[b]
                op("pe", lambda e, bk=bk, src=src: e.transpose(bk[:, 0:128], src[:, :], self.ident[:, :]), r=[kk, "ident"], w=[("ps", b)])
                op("act", lambda e, bk=bk, k=k, jj=jj: e.copy(out=Blhs[k][:, jj, :], in_=bk[:, 0:128]), r=[("ps", b)], w=["Blhs"])
        Cin = [nc.alloc_sbuf_tensor_at("Cin%d_%d" % (k, i), [128, 8, 128], F32, offset=self._off(hN) + 2048 + k * 4096) for k in range(2)]
        for k, Csrc in enumerate((Cre, Cim)):
            op("dve", lambda e, k=k: e.memset(Cin[k][:, :, :], 0.0), w=[("Cin", k)])
            op("pool", lambda e, k=k: e.memset(Clhs[k][:, :, :], 0.0), w=["Clhs"])
            Cv = Csrc.rearrange("(j4 jm gl) c p -> gl jm c j4 p", jm=4, gl=2)
            for gl in range(2):
                for jm in range(4):
                    op("sp", lambda e, k=k, gl=gl, jm=jm, Cv=Cv: e.dma_start(
                        out=Cin[k][32 * jm + 16 * gl:32 * jm + 16 * gl + 16, :, 64 * gl:64 * gl + 64], in_=Cv[gl, jm]), w=[("Cin", k)], dma=True)
            for j4 in range(8):
                b = self.bank()
                bk = self.banks[b]
                op("pe", lambda e, bk=bk, k=k, j4=j4: e.transpose(bk[:, 0:128], Cin[k][:, j4, :], self.ident[:, :]), r=[("Cin", k), "ident"], w=[("ps", b)])
                for jm in range(4):
                    jj = j4 * 4 + jm
                    op("act", lambda e, bk=bk, k=k, jj=jj, jm=jm: e.activation(out=Clhs[k][:, jj, 32 * jm:32 * jm + 32], in_=bk[:, 32 * jm:32 * jm + 32],
                                                                              func=AF.Copy, scale=(1.0 if k == 0 else -1.0)), r=[("ps", b)], w=["Clhs"])
        for ct in range(8):
            op("dve", lambda e, ct=ct: e.tensor_scalar(out=Dg[:, ct, :], in0=self.ident[:, :], scalar1=self.vcol(36 + j, ct), scalar2=None, op0=ALU.mult),
               r=["ident", "vT"], w=["Dg"])
        P.barrier()

        s5tiles = [(k * SW, SW) for k in range(NTP // SW)] + [(NTP, NTS)]
        ybk = [0]
        YK = [("yT", c_) for c_ in range(8)]

        def subtile(t0y, T_, st_r, st_i, kst, cTn, sTn):
            def stA_pe(ct):
                p = ct % 2
                bA, bB = 2 * p, 2 * p + 1
                bkA, bkB = self.banks[bA], self.banks[bB]
                for jm in range(4):
                    k = 4 * ct + jm
                    op("pe", lambda e, k=k, jm=jm: e.matmul(bkA[:, jm * T_:(jm + 1) * T_], Blhs[0][:, k, :], yT[:, ct, t0y:t0y + T_], start=True, stop=True),
                       r=["Blhs", ("yT", ct)], w=[("ps", bA)])
                for jm in range(4):
                    k = 4 * ct + jm
                    op("pe", lambda e, k=k, jm=jm: e.matmul(bkB[:, jm * T_:(jm + 1) * T_], Blhs[1][:, k, :], yT[:, ct, t0y:t0y + T_], start=True, stop=True),
                       r=["Blhs", ("yT", ct)], w=[("ps", bB)])

            def stA_ev(ct):
                p = ct % 2
                bA, bB = 2 * p, 2 * p + 1
                bkA, bkB = self.banks[bA], self.banks[bB]
                vA = bkA[:, 0:4 * T_].rearrange("p (a b) -> p a b", a=4)
                vB = bkB[:, 0:4 * T_].rearrange("p (a b) -> p a b", a=4)
                rc = rotc[:, 4 * ct:4 * ct + 4, 0:T_]
                rs = rots[:, 4 * ct:4 * ct + 4, 0:T_]
                t1, t2, t3, t4 = (tt_[n_][p][:, :, 0:T_] for n_ in ("t1", "t2", "t3", "t4"))
                op("dve", lambda e: e.tensor_tensor(out=t1, in0=vA, in1=rc, op=ALU.mult), r=[("ps", bA), "rot"], w=[("s5t", p, 1)])
                op("dve", lambda e: e.tensor_tensor(out=t4, in0=vA, in1=rs, op=ALU.mult), r=[("ps", bA), "rot"], w=[("s5t", p, 4)])
                op("dve", lambda e: e.tensor_tensor(out=t2, in0=vB, in1=rs, op=ALU.mult), r=[("ps", bB), "rot"], w=[("s5t", p, 2)])
                op("dve", lambda e: e.tensor_tensor(out=t3, in0=vB, in1=rc, op=ALU.mult), r=[("ps", bB), "rot"], w=[("s5t", p, 3)])
                bmr, bmi = wk_["bmr"][p][:, :, 0:T_], wk_["bmi"][p][:, :, 0:T_]
                op("dve", lambda e: e.tensor_tensor(out=bmr, in0=t1, in1=t2, op=ALU.add), r=[("s5t", p, 1), ("s5t", p, 2)], w=[("s5bm", p, 0)])
                op("dve", lambda e: e.tensor_tensor(out=bmi, in0=t3, in1=t4, op=ALU.subtract), r=[("s5t", p, 3), ("s5t", p, 4)], w=[("s5bm", p, 1)])

            def stC(ct):
                p = ct % 2
                wr, wi = wk_["wr"][p], wk_["wi"][p]
                bmr, bmi = wk_["bmr"][p], wk_["bmi"][p]
                for jm in range(4):
                    k = 4 * ct + jm
                    op("dve", lambda e, k=k, jm=jm: e.tensor_tensor_scan(out=wr[:, jm, 0:T_], data0=sm["mag"][:, k:k + 1].to_broadcast([128, T_]),
                                                                         data1=bmr[:, jm, 0:T_], initial=st_r[:, k:k + 1], op0=ALU.mult, op1=ALU.add),
                       r=[("s5bm", p, 0), "s_mag", kst], w=[("s5w", p, 0, jm)])
                    op("dve", lambda e, k=k, jm=jm: e.tensor_tensor_scan(out=wi[:, jm, 0:T_], data0=sm["mag"][:, k:k + 1].to_broadcast([128, T_]),
                                                                         data1=bmi[:, jm, 0:T_], initial=st_i[:, k:k + 1], op0=ALU.mult, op1=ALU.add),
                       r=[("s5bm", p, 1), "s_mag", kst], w=[("s5w", p, 1, jm)])
                kwr = [("s5w", p, 0, jm) for jm in range(4)]
                kwi = [("s5w", p, 1, jm) for jm in range(4)]
                op("act", lambda e: e.copy(out=sm["wlr"][:, 4 * ct:4 * ct + 4], in_=wr[:, :, T_ - 1]), r=kwr, w=[("wl", 0, ct)])
                op("act", lambda e: e.copy(out=sm["wli"][:, 4 * ct:4 * ct + 4], in_=wi[:, :, T_ - 1]), r=kwi, w=[("wl", 1, ct)])
                rc = rotc[:, 4 * ct:4 * ct + 4, 0:T_]
                rs = rots[:, 4 * ct:4 * ct + 4, 0:T_]
                q = ct % 2
                d1, d2, d3, d4 = (dd[n_][q] for n_ in ("d1", "d2", "d3", "d4"))
                op("dve", lambda e: e.tensor_tensor(out=d1[:, :, 0:T_], in0=wr[:, :, 0:T_], in1=rc, op=ALU.mult), r=kwr + ["rot"], w=[("s5d", q, 1)])
                op("dve", lambda e: e.tensor_tensor(out=d3[:, :, 0:T_], in0=wr[:, :, 0:T_], in1=rs, op=ALU.mult), r=kwr + ["rot"], w=[("s5d", q, 3)])
                op("dve", lambda e: e.scalar_tensor_tensor(out=d2[:, :, 0:T_], in0=wi[:, :, 0:T_], scalar=-1.0, in1=rs, op0=ALU.mult, op1=ALU.mult),
                   r=kwi + ["rot"], w=[("s5d", q, 2)])
                op("dve", lambda e: e.tensor_tensor(out=d4[:, :, 0:T_], in0=wi[:, :, 0:T_], in1=rc, op=ALU.mult), r=kwi + ["rot"], w=[("s5d", q, 4)])
                by = 6 + (ybk[0] % 2)
                ybk[0] += 1
                bky = self.banks[by]
                for q_, (dq, cl) in enumerate(((d1, 0), (d3, 1), (d2, 0), (d4, 1))):
                    for jm in range(4):
                        k = 4 * ct + jm
                        op("pe", lambda e, dq=dq, cl=cl, q_=q_, k=k, jm=jm: e.matmul(bky[:, 0:T_], Clhs[cl][:, k, :], dq[:, jm, 0:T_],
                                                                                    start=(jm == 0 and q_ == 0), stop=False),
                           r=[("s5d", q, (1, 3, 2, 4)[q_]), "Clhs"], w=[("ps", by)])
                op("pe", lambda e: e.matmul(bky[:, 0:T_], Dg[:, ct, :], yT[:, ct, t0y:t0y + T_], start=False, stop=True), r=["Dg", ("yT", ct)], w=[("ps", by)])
                op("act", lambda e: e.copy(out=yT[:, ct, t0y:t0y + T_], in_=bky[:, 0:T_]), r=[("ps", by)], w=[("yT", ct)])

            stA_pe(0)
            stA_pe(1)
            for ct in range(9):
                if ct < 8:
                    stA_ev(ct)
                if ct + 2 < 8:
                    stA_pe(ct + 2)
                if ct >= 1:
                    stC(ct - 1)
            wlk = [("wl", a_, c_) for a_ in range(2) for c_ in range(8)]
            self.cmul(st_r[:, :], st_i[:, :], sm["wlr"][:, :], sm["wli"][:, :], cTn[:, :], sTn[:, :], sm["t1"][:, :], wlk + ["s_rT", "s_r32", kst], kst)

        op("dve", lambda e: e.memset(sm["str"][:, :], 0.0), w=["st"])
        op("dve", lambda e: e.memset(sm["sti"][:, :], 0.0), w=["st"])

        def inproj(t0, W):
            self.norm_to(lambda dt: hN[:, dt, 0:W], ["hN"], t0, W, i * 4 + 1, tmp, t0 // 512)

            def cons_u(mt, k, ba, bb):
                bk = self.banks[ba]
                op("act", lambda e: e.copy(out=yT[:, mt, t0:t0 + W], in_=bk[:, 0:W]), r=[("ps", ba)], w=[("yT", mt)])
            self.dual_linear(Win, None, D, hN, ["hN"], [(0, W)], cons_u, [(w_[0], None) for w_ in wbufs], "s5w", CB=128, nb=4, b0=2)

        inproj(*s5tiles[0])
        for ki, (t0, W) in enumerate(s5tiles):
            if ki + 1 < len(s5tiles):
                inproj(*s5tiles[ki + 1])
            if t0 < NTP:
                for sb in range(W // T):
                    subtile(t0 + sb * T, T, sm["str"], sm["sti"], "st", sm["cT"], sm["sT"])
            else:
                subtile(t0, W, sm["ssr"], sm["ssi"], "ss", sm["c32"], sm["s32"])
        small_out(sm["ssr"], o_shre, "ss")
        small_out(sm["ssi"], o_shim, "ss")
        P.barrier()
        op("dve", lambda e: e.tensor_copy(out=st64[:, 0:32], in_=sm["str"][:, :]), r=["st"], w=["xsrc"])
        op("dve", lambda e: e.tensor_copy(out=st64[:, 32:64], in_=sm["sti"][:, :]), r=["st"], w=["xsrc"])
        if KD in (30, 31):
            op("dve", lambda e: e.memset(gath[:, :, :], 0.0), w=["xdst"])
        else:
            self.exchange(st64[:, :], 64, gath)
        for kk_, (ar, ai) in enumerate((("a1r", "a1i"), ("a2r", "a2i"), ("a3r", "a3i"))):
            for (an, lo) in ((ar, 0), (ai, 32)):
                for r in range(8):
                    col = kk_ * 8 + r
                    if r == 0:
                        op("dve", lambda e, an=an, lo=lo, col=col: e.tensor_scalar(out=sm[an][:, :], in0=gath[:, 0, lo:lo + 32], scalar1=self.cm[:, col:col + 1],
                                                                                  scalar2=None, op0=ALU.mult), r=["xdst", "cm", "s_c"], w=["s_A"])
                    else:
                        op("dve", lambda e, an=an, lo=lo, col=col, r=r: e.scalar_tensor_tensor(out=sm[an][:, :], in0=gath[:, r, lo:lo + 32],
                                                                                              scalar=self.cm[:, col:col + 1], in1=sm[an][:, :],
                                                                                              op0=ALU.mult, op1=ALU.add), r=["xdst", "cm", "s_A"], w=["s_A"])
        self.cmul(S("hir"), S("hii"), S("a2r"), S("a2i"), S("p1r"), S("p1i"), S("t1"), ["s_A", "s_p1"], "s_hi")
        self.cmul(S("s2r"), S("s2i"), S("a3r"), S("a3i"), S("p2r"), S("p2i"), S("t1"), ["s_A", "s_p2"], "s_s2")
        for (o_, a_, b_) in (("hir", "a1r", "s2r"), ("hii", "a1i", "s2i")):
            op("dve", lambda e, o_=o_, a_=a_: e.tensor_tensor(out=S(o_), in0=S(o_), in1=S(a_), op=ALU.add), r=["s_hi", "s_A"], w=["s_hi"])
            op("dve", lambda e, o_=o_, b_=b_: e.tensor_tensor(out=S(o_), in0=S(o_), in1=S(b_), op=ALU.add), r=["s_hi", "s_s2"], w=["s_hi"])
        self.cmul(S("fnr"), S("fni"), S("hir"), S("hii"), S("p1r"), S("p1i"), S("t1"), ["s_hi", "s_p1"], "s_fn")
        op("dve", lambda e: e.tensor_tensor(out=S("fnr"), in0=S("fnr"), in1=S("str"), op=ALU.add), r=["s_fn", "st"], w=["s_fn"])
        op("dve", lambda e: e.tensor_tensor(out=S("fni"), in0=S("fni"), in1=S("sti"), op=ALU.add), r=["s_fn", "st"], w=["s_fn"])
        small_out(sm["fnr"], o_hre, "s_fn")
        small_out(sm["fni"], o_him, "s_fn")
        op("dve", lambda e: e.tensor_tensor(out=S("a2i"), in0=S("are"), in1=S("dtv"), op=ALU.mult), r=["s_are", "s_dtv", "s_A"], w=["s_a2i"])
        op("act", lambda e: e.activation(out=S("a2i"), in_=S("a2i"), func=AF.Exp, scale=float(T)), r=["s_a2i"], w=["s_a2i"])
        op("dve", lambda e: e.tensor_tensor(out=S("lTr"), in0=S("cT"), in1=S("a2i"), op=ALU.mult), r=["s_rT", "s_a2i"], w=["s_lT"])
        op("dve", lambda e: e.tensor_tensor(out=S("lTi"), in0=S("sT"), in1=S("a2i"), op=ALU.mult), r=["s_rT", "s_a2i"], w=["s_lT"])
        for k in range(32):
            rp = rpw2[k % 2]
            op("dve", lambda e, k=k, rp=rp: e.tensor_tensor_scan(out=rp[:, :], data0=sm["mag"][:, k:k + 1].to_broadcast([128, T]), data1=self.zeros[:, 0:T],
                                                                initial=1.0, op0=ALU.mult, op1=ALU.add), r=["s_mag", "zeros"] + YK, w=[("rpw", k % 2)])
            op("dve", lambda e, k=k, rp=rp: e.tensor_tensor(out=rotc[:, k, :], in0=rotc[:, k, :], in1=rp[:, :], op=ALU.mult), r=[("rpw", k % 2), "rot"], w=["rot"])
            op("dve", lambda e, k=k, rp=rp: e.tensor_tensor(out=rots[:, k, :], in0=rots[:, k, :], in1=rp[:, :], op=ALU.mult), r=[("rpw", k % 2), "rot"], w=["rot"])
        hk = [("hir", "hii"), ("hnr", "hni")]
        p2i = [0]
        for sb in range(0 if KD == 31 else NTP // T):
            hr_, hi_ = hk[sb % 2]
            nr_, ni_ = hk[(sb + 1) % 2]
            op("dve", lambda e, hi_=hi_: e.tensor_scalar(out=S("nhi"), in0=S(hi_), scalar1=-1.0, scalar2=None, op0=ALU.mult), r=["s_hi"], w=["s_nhi"])

            def p2_ct(ct, hr_=hr_, hi_=hi_, sb=sb):
                q = p2i[0] % 2
                p2i[0] += 1
                if q == 0:
                    d1, d2, d3, d4 = (dd[n_][0] for n_ in ("d1", "d2", "d3", "d4"))
                else:
                    d1, d2, d3, d4 = (tt_[n_][0] for n_ in ("t1", "t2", "t3", "t4"))
                rc = rotc[:, 4 * ct:4 * ct + 4, :]
                rs = rots[:, 4 * ct:4 * ct + 4, :]
                op("dve", lambda e: e.tensor_tensor(out=d1[:, :, :], in0=rc, in1=sm[hr_][:, 4 * ct:4 * ct + 4].unsqueeze(2).to_broadcast([128, 4, T]), op=ALU.mult),
                   r=["rot", "s_hi"], w=[("p2d", q, 1)])
                op("dve", lambda e: e.tensor_tensor(out=d3[:, :, :], in0=rs, in1=sm[hr_][:, 4 * ct:4 * ct + 4].unsqueeze(2).to_broadcast([128, 4, T]), op=ALU.mult),
                   r=["rot", "s_hi"], w=[("p2d", q, 3)])
                for jm in range(4):
                    k = 4 * ct + jm
                    op("act", lambda e, k=k, jm=jm: e.activation(out=d2[:, jm, :], in_=rots[:, k, :], func=AF.Copy, scale=sm["nhi"][:, k:k + 1]),
                       r=["rot", "s_nhi"], w=[("p2d", q, 2)])
                    op("act", lambda e, k=k, jm=jm: e.activation(out=d4[:, jm, :], in_=rotc[:, k, :], func=AF.Copy, scale=sm[hi_][:, k:k + 1]),
                       r=["rot", "s_hi"], w=[("p2d", q, 4)])
                by = 6 + (ybk[0] % 2)
                ybk[0] += 1
                bky = self.banks[by]
                for q_, (dq, cl) in enumerate(((d1, 0), (d3, 1), (d2, 0), (d4, 1))):
                    for jm in range(4):
                        k = 4 * ct + jm
                        op("pe", lambda e, dq=dq, cl=cl, q_=q_, k=k, jm=jm: e.matmul(bky[:, 0:T], Clhs[cl][:, k, :], dq[:, jm, :],
                                                                                    start=(jm == 0 and q_ == 0), stop=(jm == 3 and q_ == 3)),
                           r=[("p2d", q, (1, 3, 2, 4)[q_]), "Clhs"], w=[("ps", by)])
                t0y = sb * T
                op("dve", lambda e: e.tensor_tensor(out=yT[:, ct, t0y:t0y + T], in0=bky[:, 0:T], in1=yT[:, ct, t0y:t0y + T], op=ALU.add),
                   r=[("ps", by), ("yT", ct)], w=[("yT", ct)])
            for ct in range(8):
                p2_ct(ct)
            self.cmul(S(nr_), S(ni_), S(hr_), S(hi_), S("lTr"), S("lTi"), S("t1"), ["s_hi", "s_lT"], "s_hi")
        if KD == 30:
            dbg = self.dout("dbg", [128, 24, 32])
            for n_i, n_ in enumerate(("dtv", "mag", "th", "lr", "li", "cre", "cim", "cT", "sT", "c32", "s32", "p1r", "p1i", "p2r", "p2i", "str", "sti", "ssr", "ssi", "wlr", "wli", "hir", "fnr", "are")):
                self.store(dbg[:, n_i, :], sm[n_][:, :], ["st", "ss", "s_hi", "s_fn", "wl"])
            dbg2 = self.dout("dbg2", [128, 8, 128])
            dbt = nc.alloc_sbuf_tensor_at("dbt_%d" % i, [128, 8, 128], F32, offset=self._off(hN))
            op("dve", lambda e: e.tensor_copy(out=dbt[:, 0, :], in_=rotc[:, 1, :]), r=["rot"], w=["dbt"])
            op("dve", lambda e: e.tensor_copy(out=dbt[:, 1, :], in_=rots[:, 1, :]), r=["rot"], w=["dbt"])
            op("dve", lambda e: e.tensor_copy(out=dbt[:, 2, :], in_=Blhs[0][:, 1, :]), r=["Blhs"], w=["dbt"])
            op("dve", lambda e: e.tensor_copy(out=dbt[:, 3, :], in_=Blhs[1][:, 1, :]), r=["Blhs"], w=["dbt"])
            op("dve", lambda e: e.tensor_copy(out=dbt[:, 4, :], in_=Clhs[0][:, 1, :]), r=["Clhs"], w=["dbt"])
            op("dve", lambda e: e.tensor_copy(out=dbt[:, 5, :], in_=Clhs[1][:, 1, :]), r=["Clhs"], w=["dbt"])
            op("dve", lambda e: e.tensor_copy(out=dbt[:, 6, :], in_=yT[:, 0, 0:128]), r=YK, w=["dbt"])
            op("dve", lambda e: e.tensor_copy(out=dbt[:, 7, :], in_=yT[:, 0, NTP:NTP + 128] if False else yT[:, 1, 0:128]), r=YK, w=["dbt"])
            self.store(dbg2[:, :, :], dbt[:, :, :], ["dbt"])
        P.barrier()
        for (t0, W) in s5tiles:
            self.glu_tile(t0 // 512, t0, W, yT, Wglu, wbufs, sgl, mix, tmp, i)

    def glu_tile(self, ti, t0, W, yT, Wglu, wbufs, sgl, mix, tmp, i):
        op = self.P.op
        for dt in range(8):
            op("act", lambda e, dt=dt: e.activation(out=yT[:, dt, t0:t0 + W], in_=yT[:, dt, t0:t0 + W], func=AF.Gelu), r=[("yT", dt)], w=[("yT", dt)])
        sgi = [0]

        def cons(mt, k, ba, bb):
            bka, bkb = self.banks[ba], self.banks[bb]
            sg = sgl[sgi[0] % 2]
            ks = ("sgl", sgi[0] % 2)
            sgi[0] += 1
            op("act", lambda e: e.activation(out=sg[:, 0:W], in_=bkb[:, 0:W], func=AF.Sigmoid), r=[("ps", bb)], w=[ks])
            op("dve", lambda e: e.tensor_tensor(out=mix[:, mt, 0:W], in0=bka[:, 0:W], in1=sg[:, 0:W], op=ALU.mult), r=[("ps", ba), ks], w=["mix"])
        self.dual_linear(Wglu[:, 0:D], Wglu[:, D:2 * D], D, yT, [("yT", c_) for c_ in range(8)], [(t0, W)], cons, wbufs, "s5w", CB=128)
        self.resid_update(mix, ["mix"], t0, W, 16 + i * 4 + 1, tmp, ti)

    def _off(self, t):
        return t.manual_sbuf_range[0]


_INPUT_ORDER = None


def _in_maps(inp):
    f = lambda a: np.ascontiguousarray(np.asarray(a, dtype=np.float32))
    vecs = np.concatenate([
        f(inp["norm_pre"]).reshape(16, D), f(inp["norm_post"]).reshape(16, D), f(inp["norm_mem"]).reshape(4, D),
        f(inp["s5_d"]).reshape(2, D), f(inp["pool_b_grp"]).reshape(2, D), f(inp["pool_scale"]).reshape(2, D)], axis=0)
    shared = {
        "vecs": np.ascontiguousarray(vecs),
        "ffn_w_gate": f(inp["ffn_w_gate"]), "ffn_w_up": f(inp["ffn_w_up"]), "ffn_w_down": f(inp["ffn_w_down"]),
        "xa_w_q": f(inp["xa_w_q"]), "xa_w_k": f(inp["xa_w_k"]), "xa_w_v": f(inp["xa_w_v"]), "xa_w_o": f(inp["xa_w_o"]),
        "s5_w_in": f(inp["s5_w_in"]), "s5_a_re": f(inp["s5_a_re"]).reshape(2, 32, 128), "s5_a_im": f(inp["s5_a_im"]).reshape(2, 32, 128),
        "s5_b_re": f(inp["s5_b_re"]), "s5_b_im": f(inp["s5_b_im"]), "s5_c_re": f(inp["s5_c_re"]), "s5_c_im": f(inp["s5_c_im"]),
        "s5_log_dt": np.ascontiguousarray(f(inp["s5_log_dt"]).reshape(2, 32, 2).transpose(0, 2, 1)),
        "s5_w_glu": f(inp["s5_w_glu"]), "pool_w_in": f(inp["pool_w_in"]), "pool_w_grp": f(inp["pool_w_grp"]),
    }
    xp = f(inp["x_prompt"]); xs = f(inp["x_sample"]); ck = f(inp["cache_mem_k"]); cv = f(inp["cache_mem_v"])
    sre = f(inp["state_ssm_re"]); sim = f(inp["state_ssm_im"]); spool = f(inp["state_pool"]); memp = f(inp["mem_prompt"])
    maps = []
    for c in range(8):
        b, q = c // 4, c % 4
        cm = np.zeros((128, 40), np.float32)
        for r in range(8):
            if r // 4 == b and r < c:
                cm[:, (c - r - 1) * 8 + r] = 1.0
        if q > 0:
            cm[:, 24 + c - 1] = 1.0
        cm[:, 32] = float(q * NTP)
        m = dict(shared)
        m.update({
            "xp": np.ascontiguousarray(xp[b, q * NTP:(q + 1) * NTP]), "xs": np.ascontiguousarray(xs[c]),
            "ck": np.ascontiguousarray(ck[:, c].reshape(4, 256, D)), "cv": np.ascontiguousarray(cv[:, c].reshape(4, 256, D)),
            "sre": np.ascontiguousarray(sre[:, c].reshape(2, 32, 128)), "sim": np.ascontiguousarray(sim[:, c].reshape(2, 32, 128)),
            "spool": np.ascontiguousarray(spool[:, c]), "memp": np.ascontiguousarray(memp[b]), "cmask": cm,
        })
        maps.append(m)
    return maps


def run(inp, plan=None, cores=8, trace=False):
    bld = Builder(stop_after=plan)
    nc = bld.build()
    maps = _in_maps(inp)[:cores]
    maps = [{k: v for k, v in m.items() if k in bld.ins} for m in maps]
    res = run_bass_kernel_spmd(nc, maps, core_ids=list(range(cores)), trace=trace)
    return res


def kernel(**inp):
    res = run(inp).results
    y_prompt = np.stack([np.concatenate([res[b * 4 + q]["yp"] for q in range(4)], axis=0) for b in range(2)])
    y_sample = np.stack([res[c]["ys"] for c in range(8)])
    mkp = np.stack([res[b * 4]["mk"] for b in range(2)], axis=1).reshape(4, 2, 256, 4, 256)
    mvp = np.stack([res[b * 4]["mv"] for b in range(2)], axis=1).reshape(4, 2, 256, 4, 256)
    hre = np.stack([res[b * 4 + 3]["o_hre"] for b in range(2)], axis=1).reshape(2, 2, 64, 64)
    him = np.stack([res[b * 4 + 3]["o_him"] for b in range(2)], axis=1).reshape(2, 2, 64, 64)
    pp = np.stack([res[b * 4 + 3]["o_pool"] for b in range(2)], axis=1)
    shre = np.stack([res[c]["o_shre"] for c in range(8)], axis=1).reshape(2, 8, 64, 64)
    shim = np.stack([res[c]["o_shim"] for c in range(8)], axis=1).reshape(2, 8, 64, 64)
    sp = np.stack([res[c]["o_spool"] for c in range(8)], axis=1)
    return tuple(np.ascontiguousarray(a.astype(np.float32)) for a in (y_prompt, y_sample, mkp, mvp, hre, him, pp, shre, shim, sp))
```
